# Optimizing a Trainium2 kernel written in Bass

```python
import jax, jax.numpy as jnp
from jax import lax
import numpy as np

D_MODEL = 2048
BATCH = 4
SEQ = 4096
DEPTH = 4

N_MIXERS = 2
N_CONV_LAYERS = (DEPTH + 1) // 2
N_POOL_LAYERS = DEPTH // 2
CONV_WIDTH = 3
POOL_WINDOWS = (2, 4, 8, 16)
N_POOL_GROUPS = len(POOL_WINDOWS)
GROUP_DIM = D_MODEL // N_POOL_GROUPS
D_FF = 5632
N_MOD = 9
EPS = 1e-6
FFN_RES_WEIGHT = 0.5

kernel_name = "hybrid_shortconv_pool_macaron_adaln"


def rmsnorm(x, g):
    xf = x.astype(jnp.float32)
    y = xf * lax.rsqrt(jnp.mean(xf * xf, axis=-1, keepdims=True) + EPS)
    return (y * g.astype(jnp.float32)).astype(x.dtype)


def modulate(h, shift, scale):
    return h * (1 + scale[:, None, :]) + shift[:, None, :]


def swiglu(h, w_in, w_out):
    gu = jnp.einsum('bsd,df->bsf', h, w_in)
    g, u = jnp.split(gu, 2, axis=-1)
    return jnp.einsum('bsf,fd->bsd', jax.nn.silu(g) * u, w_out)


def causal_shift(u, k):
    pad = [(0, 0)] * u.ndim
    pad[1] = (k, 0)
    return jnp.pad(u, pad)[:, :u.shape[1]]


def short_conv_mixer(h, w_in, w_conv, w_out):
    proj = jnp.einsum('bsd,de->bse', h, w_in)
    b_gate, c_gate, v = jnp.split(proj, 3, axis=-1)
    u = c_gate * v
    y = sum(w_conv[CONV_WIDTH - 1 - k] * causal_shift(u, k) for k in range(CONV_WIDTH))
    return jnp.einsum('bsd,de->bse', b_gate * y, w_out)


def pooling_mixer(h, w_group, scale):
    B, S, D = h.shape
    hg = h.reshape(B, S, N_POOL_GROUPS, GROUP_DIM).astype(jnp.float32)
    cs = jnp.cumsum(hg, axis=1)
    t = jnp.arange(S)
    outs = []
    for g, w in enumerate(POOL_WINDOWS):
        csg = cs[:, :, g]
        win_sum = csg - causal_shift(csg, w)
        count = jnp.minimum(t + 1, w).astype(jnp.float32)[None, :, None]
        outs.append(win_sum / count - hg[:, :, g])
    pooled = jnp.stack(outs, axis=2).astype(h.dtype)
    y = jnp.einsum('bsgc,gce->bsge', pooled, w_group).reshape(B, S, D)
    return y * scale


def setup_inputs(seed: int = 0) -> dict:
    key = jax.random.key(seed)
    ks = jax.random.split(key, 20)
    D, F = D_MODEL, D_FF
    f32 = jnp.float32

    def nrm(k, shape, std):
        return jax.random.normal(k, shape, f32) * std

    return {
        "x": nrm(ks[0], (BATCH, SEQ, D), 1.0),
        "c": nrm(ks[1], (BATCH, D), 1.0),
        "norm_ffn1": 1.0 + nrm(ks[2], (DEPTH, D), 0.05),
        "norm_mix": 1.0 + nrm(ks[3], (DEPTH, D), 0.05),
        "norm_ffn2": 1.0 + nrm(ks[4], (DEPTH, D), 0.05),
        "w_ada": nrm(ks[5], (DEPTH, D, N_MOD * D), 0.5 * D ** -0.5),
        "b_ada": nrm(ks[6], (DEPTH, N_MOD * D), 0.02),
        "w_ffn1_in": nrm(ks[7], (DEPTH, D, 2 * F), D ** -0.5),
        "w_ffn1_out": nrm(ks[8], (DEPTH, F, D), F ** -0.5),
        "w_ffn2_in": nrm(ks[9], (DEPTH, D, 2 * F), D ** -0.5),
        "w_ffn2_out": nrm(ks[10], (DEPTH, F, D), F ** -0.5),
        "conv_in": nrm(ks[11], (N_CONV_LAYERS, D, 3 * D), D ** -0.5),
        "conv_w": nrm(ks[12], (N_CONV_LAYERS, CONV_WIDTH, D), CONV_WIDTH ** -0.5),
        "conv_out": nrm(ks[13], (N_CONV_LAYERS, D, D), D ** -0.5),
        "pool_w": nrm(ks[14], (N_POOL_LAYERS, N_POOL_GROUPS, GROUP_DIM, GROUP_DIM), GROUP_DIM ** -0.5),
        "pool_scale": 1.0 + nrm(ks[15], (N_POOL_LAYERS, D), 0.1),
        "final_norm": 1.0 + nrm(ks[16], (D,), 0.05),
    }


def reference(x, c, norm_ffn1, norm_mix, norm_ffn2, w_ada, b_ada, w_ffn1_in, w_ffn1_out,
              w_ffn2_in, w_ffn2_out, conv_in, conv_w, conv_out, pool_w, pool_scale, final_norm):
    c_act = jax.nn.silu(c)
    for i in range(DEPTH):
        mods = jnp.einsum('bd,de->be', c_act, w_ada[i]) + b_ada[i]
        sh1, sc1, g1, sh2, sc2, g2, sh3, sc3, g3 = jnp.split(mods, N_MOD, axis=-1)

        h = modulate(rmsnorm(x, norm_ffn1[i]), sh1, sc1)
        x = x + FFN_RES_WEIGHT * g1[:, None, :] * swiglu(h, w_ffn1_in[i], w_ffn1_out[i])

        h = modulate(rmsnorm(x, norm_mix[i]), sh2, sc2)
        j = i // N_MIXERS
        if i % N_MIXERS == 0:
            m = short_conv_mixer(h, conv_in[j], conv_w[j], conv_out[j])
        else:
            m = pooling_mixer(h, pool_w[j], pool_scale[j])
        x = x + g2[:, None, :] * m

        h = modulate(rmsnorm(x, norm_ffn2[i]), sh3, sc3)
        x = x + FFN_RES_WEIGHT * g3[:, None, :] * swiglu(h, w_ffn2_in[i], w_ffn2_out[i])
    return rmsnorm(x, final_norm)
```

```python
import contextlib
import numpy as np
import ml_dtypes
import concourse.bass as bass
import concourse.mybir as mybir
from concourse.bass_utils import run_bass_kernel_spmd

F32 = mybir.dt.float32
BF16 = mybir.dt.bfloat16
AF = mybir.ActivationFunctionType
ALU = mybir.AluOpType

D = 2048
BATCH = 4
SEQ = 4096
DEPTH = 4
FF = 5632
NK = D // 128
NFC = FF // 128
G = 4
NG = NFC // G
NMOD = 9
EPS = 1e-6
POOL_W = (2, 4, 8, 16)
HALO = 34
NCORES = 8
NT = 2
OWN = (SEQ // 2) // NT
TTR = OWN + HALO
TT = TTR + 1
NSUB = -(-TT // 512)
_b = TT // NSUB
_r = TT % NSUB
SUBS = []
_o = 0
for _i in range(NSUB):
    _n = _b + (1 if _i < _r else 0)
    SUBS.append((_o, _n))
    _o += _n
assert all(n == SUBS[0][1] for _, n in SUBS)
SN = SUBS[0][1]
NH = SN + 1
PAD = 16
NSLOT = 3
SLOT_ELEMS = 8192
NVEC = 21
ICW = 64


class Buf:
    __slots__ = ("w", "r")

    def __init__(self):
        self.w = None
        self.r = []


class Prog:
    ENGS = ("pe", "act", "dve", "pool", "sp")

    def __init__(self):
        self.streams = {e: [] for e in self.ENGS}
        self.cnt = {e: 0 for e in self.ENGS}
        self.pend = {e: [] for e in self.ENGS}
        self.waited = {e: {} for e in self.ENGS}
        self.dma_cnt = {}

    def _deps(self, eng, reads, writes, extra):
        deps = {}

        def add(tok):
            if tok is None:
                return
            k, v = tok
            if v > deps.get(k, 0):
                deps[k] = v
        for b in reads:
            add(b.w)
        for b in writes:
            add(b.w)
            for t in b.r:
                add(t)
        for t in extra:
            add(t)
        waits = []
        wd = self.waited[eng]
        for k, v in deps.items():
            if k == eng and eng in ("pe", "sp", "pool"):
                continue
            if wd.get(k, 0) < v:
                wd[k] = v
                waits.append((k, v))
        return waits

    def _assign(self, tok, reads, writes):
        for b in reads:
            b.r.append(tok)
        for b in writes:
            b.w = tok
            b.r = []

    def emit(self, eng, fn, reads=(), writes=(), inc=True, extra=()):
        waits = self._deps(eng, reads, writes, extra)
        tok = None
        incinfo = None
        if inc:
            self.cnt[eng] += 1
            tok = (eng, self.cnt[eng])
            incinfo = (eng, 1)
            for (r, w) in self.pend[eng]:
                self._assign(tok, r, w)
            self.pend[eng] = []
            self._assign(tok, reads, writes)
        else:
            self.pend[eng].append((tuple(reads), tuple(writes)))
        self.streams[eng].append((waits, fn, incinfo))
        return tok

    def dma(self, queue, fn, semkey, reads=(), writes=()):
        return self.dma_group(queue, [fn], semkey, reads, writes)

    def dma_group(self, queue, fns, semkey, reads=(), writes=()):
        waits = self._deps(queue, reads, writes, ())
        for fn in fns:
            self.dma_cnt[semkey] = self.dma_cnt.get(semkey, 0) + 16
            self.streams[queue].append((waits, fn, (semkey, 16)))
            waits = []
        tok = (semkey, self.dma_cnt[semkey])
        self._assign(tok, reads, writes)
        return tok

    def wait_all(self, eng, toks):
        waits = []
        wd = self.waited[eng]
        for (k, v) in toks:
            if wd.get(k, 0) < v:
                wd[k] = v
                waits.append((k, v))
        self.streams[eng].append((waits, None, None))


def build_program(n_layers=DEPTH):
    nc = bass.Bass("TRN2", target_bir_lowering=False)
    P = Prog()

    def din(name, shape, dt=F32):
        return nc.dram_tensor(name, list(shape), dt, kind="ExternalInput").ap()

    x_in = din("x_in", [NT, D, TT])
    maskd = din("maskd", [NT, 128, TT], BF16)
    invd = din("invd", [NT, 128, 4 * ICW])
    c_col = din("c_col", [128, NK])
    vecs_d = din("vecs", [128, NVEC * NK])
    bada_d = din("bada", [128, DEPTH * NMOD * NK])
    w_ada = din("w_ada", [DEPTH, D, NMOD * D])
    w_f1i = din("w_ffn1_in", [DEPTH, D, 2 * FF])
    w_f1o = din("w_ffn1_out", [DEPTH, FF, D])
    w_f2i = din("w_ffn2_in", [DEPTH, D, 2 * FF])
    w_f2o = din("w_ffn2_out", [DEPTH, FF, D])
    conv_in = din("conv_in", [2, D, 3 * D])
    conv_out = din("conv_out", [2, D, D])
    pool_w = din("pool_w", [2, 4, 512, 512])
    out_d = nc.dram_tensor("out", [NT, D, OWN], F32, kind="ExternalOutput").ap()

    es = contextlib.ExitStack()
    with es:
        def sb(name, shape, dt=F32):
            return es.enter_context(nc.sbuf_tensor(name, list(shape), dt))

        x_t = sb("x_t", [128, NK, TT])
        h_t = sb("h_t", [128, NK, NSUB, NH], BF16)
        a_flat = sb("a_t", [128, 2 * G * TT], BF16)
        a_t = a_flat[:, :].rearrange("p (a g t) -> p a g t", a=2, g=G, t=TT)
        a_f32 = a_flat.bitcast(F32)
        tmpB = [a_f32[:, q * (PAD + TT):(q + 1) * (PAD + TT)] for q in range(3)]
        rstd_t = sb("rstd_t", [128, TT])
        tmp = [sb(f"tmp{i}", [128, PAD + TT]) for i in range(3)]
        sg_t = [sb(f"sg{i}", [128, 512]) for i in range(4)]
        bias_t = sb("bias_t", [128, 4])
        mask_t = sb("mask_t", [128, TT], BF16)
        inv_t = sb("inv_t", [128, 4 * ICW])
        ta_t = sb("ta_t", [128, 2 * ICW])
        vec_t = sb("vec_t", [128, NVEC * NK])
        bada_t = sb("bada_t", [128, DEPTH * NMOD * NK])
        mods_t = sb("mods_t", [128, DEPTH * NMOD * NK])
        der_t = sb("der_t", [128, DEPTH * 6 * NK])
        ccol_t = sb("ccol_t", [128, NK])
        cact_t = sb("cact_t", [128, NK], BF16)
        ones_t = sb("ones_t", [128, 128])
        eps_t = sb("eps_t", [128, 1])
        row_t = sb("row_t", [1, 512])
        slots_t = [sb(f"slot{i}", [128, SLOT_ELEMS], BF16) for i in range(NSLOT)]
        banks_t = [es.enter_context(nc.psum_tensor(f"bank{i}", [128, 512], F32)) for i in range(8)]

        sem_names = list(Prog.ENGS) + [f"slot{i}" for i in range(NSLOT)] + ["xld", "misc", "st"]
        sems = {k: es.enter_context(nc.semaphore("s_" + k)) for k in sem_names}

        xb = [[Buf() for _ in SUBS] for _ in range(NK)]
        hb = [[Buf() for _ in SUBS] for _ in range(NK)]
        ab = [[[Buf() for _ in SUBS] for _ in range(G)] for _ in range(2)]
        rstdb = [Buf() for _ in SUBS]
        tmpb = [Buf() for _ in range(3)]
        sgb = [Buf() for _ in range(4)]
        biasb = [Buf() for _ in range(2)]
        maskb = Buf()
        invb = Buf()
        tab = [Buf() for _ in range(2)]
        tmpBb = [Buf() for _ in range(3)]
        vecb = Buf()
        badab = Buf()
        modsb = [Buf() for _ in range(DEPTH)]
        derb = [Buf() for _ in range(DEPTH)]
        rowb = [Buf() for _ in range(1)]
        ccolb = Buf()
        cactb = Buf()
        constb = Buf()
        slotb = [Buf() for _ in range(NSLOT)]
        bankb = [Buf() for _ in range(8)]
        outb = Buf()

        bank_rr = [0]

        def alloc_bank():
            i = bank_rr[0]
            bank_rr[0] = (i + 1) % 8
            return i

        def wv_rows(w2d):
            return w2d.rearrange("(k p) n -> p k n", p=128)

        plan = []

        NADA = NMOD * D // 512
        NUP = 3 * D // 512
        pts = []
        for _i in range(n_layers):
            pts += [("f", _i, 0, g, hf) for g in range(NG) for hf in range(2)]
            if _i % 2 == 0:
                pts += [("c", _i, jj) for jj in range(NK)]
            else:
                pts += [("p", _i, g) for g in range(4)]
            pts += [("f", _i, 1, g, hf) for g in range(NG) for hf in range(2)]
        rest = [(li, n) for li in range(n_layers) for n in range(NADA)][NUP:]
        assert len(pts) >= len(rest)
        ADA_AT = dict(zip(pts, rest))
        ada_done = set()

        def plan_ada_tile(li, n):
            wav = wv_rows(w_ada[li])
            plan.append(("ada", [
                (lambda s: s[:, 0:8192].rearrange("p (k c) -> p k c", k=NK, c=512),
                 wav[:, :, n * 512:(n + 1) * 512])]))

        def plan_ffn(wi, wo, t, i, which):
            wiv = wv_rows(wi)
            wov = wv_rows(wo)
            for g in range(NG):
                for half in range(2):
                    c0 = (g * G + 2 * half) * 128
                    plan.append(("ffn_in", [
                        (lambda s: s[:, 0:8192].rearrange("p (k t c) -> p k t c", k=NK, t=2, c=256)[:, :, 0, :],
                         wiv[:, :, c0:c0 + 256]),
                        (lambda s: s[:, 0:8192].rearrange("p (k t c) -> p k t c", k=NK, t=2, c=256)[:, :, 1, :],
                         wiv[:, :, FF + c0:FF + c0 + 256]),
                    ]))
                    if t == 0 and ("f", i, which, g, half) in ADA_AT:
                        plan_ada_tile(*ADA_AT[("f", i, which, g, half)])
                if g >= 1:
                    plan.append(("ffn_out", [
                        (lambda s: s[:, 0:8192].rearrange("p (f c) -> p f c", f=G, c=D),
                         wov[:, (g - 1) * G:(g - 1) * G + G, :])]))
            plan.append(("ffn_out", [
                (lambda s: s[:, 0:8192].rearrange("p (f c) -> p f c", f=G, c=D),
                 wov[:, (NG - 1) * G:NG * G, :])]))

        def plan_conv(j, t, i):
            civ = wv_rows(conv_in[j])
            cov = wv_rows(conv_out[j])
            for q in range(4):
                for jj in range(4 * q, 4 * q + 4):
                    plan.append(("conv_in", [
                        ((lambda s, tt=tt: s[:, 0:6144].rearrange("p (k t c) -> p k t c", k=NK, t=3, c=128)[:, :, tt, :]),
                         civ[:, :, tt * D + jj * 128:tt * D + jj * 128 + 128]) for tt in range(3)]))
                    if t == 0 and ("c", i, jj) in ADA_AT:
                        plan_ada_tile(*ADA_AT[("c", i, jj)])
                if q >= 1:
                    plan.append(("conv_out", [
                        (lambda s: s[:, 0:8192].rearrange("p (f c) -> p f c", f=G, c=D),
                         cov[:, (q - 1) * 4:(q - 1) * 4 + 4, :])]))
            plan.append(("conv_out", [
                (lambda s: s[:, 0:8192].rearrange("p (f c) -> p f c", f=G, c=D),
                 cov[:, 12:16, :])]))

        def plan_pool(j, t, i):
            for g in range(4):
                plan.append(("pool_w", [
                    (lambda s: s[:, 0:2048].rearrange("p (f c) -> p f c", f=4, c=512),
                     pool_w[j, g].rearrange("(k p) n -> p k n", p=128))]))
                if t == 0 and ("p", i, g) in ADA_AT:
                    plan_ada_tile(*ADA_AT[("p", i, g)])

        for n in range(NUP):
            plan_ada_tile(0, n)
        for _t in range(NT):
            for i in range(n_layers):
                plan_ffn(w_f1i[i], w_f1o[i], _t, i, 0)
                if i % 2 == 0:
                    plan_conv(i // 2, _t, i)
                else:
                    plan_pool(i // 2, _t, i)
                plan_ffn(w_f2i[i], w_f2o[i], _t, i, 1)

        ws_state = {"issued": 0, "next": 0}

        def ws_get(kind):
            i = ws_state["next"]
            assert plan[i][0] == kind, (plan[i][0], kind, i)
            ws_state["next"] = i + 1
            while ws_state["issued"] < min(len(plan), i + NSLOT):
                j = ws_state["issued"]
                sl = j % NSLOT
                fns = []
                for (dstf, src) in plan[j][1]:
                    dst = dstf(slots_t[sl])
                    fns.append(lambda e, dst=dst, src=src: e.dma_start(out=dst, in_=src))
                P.dma_group("pool", fns, f"slot{sl}", writes=[slotb[sl]])
                ws_state["issued"] = j + 1
            sl = i % NSLOT
            return slots_t[sl], slotb[sl]

        def sl_(s):
            o, n = SUBS[s]
            return slice(o, o + n)

        def vcol(v, k):
            return vec_t[:, v * NK + k:v * NK + k + 1]

        def mcol(i, m, k):
            c = (i * NMOD + m) * NK + k
            return mods_t[:, c:c + 1]

        def dcol(i, m, k):
            c = (i * 6 + m) * NK + k
            return der_t[:, c:c + 1]

        V_NF1 = lambda i: 3 * i
        V_NM = lambda i: 3 * i + 1
        V_NF2 = lambda i: 3 * i + 2
        V_FIN = 12
        V_CW = lambda j, kk: 13 + 3 * j + kk
        V_PS = lambda j: 19 + j

        P.emit("dve", lambda e: e.memset(ones_t[:, :], 1.0), writes=[constb])
        P.emit("dve", lambda e: e.memset(eps_t[:, :], EPS), writes=[constb])
        for i in range(3):
            P.emit("dve", lambda e, i=i: e.memset(tmp[i][:, 0:PAD], 0.0), writes=[tmpb[i]])
        P.dma("sp", lambda e: e.dma_start(out=vec_t[:, :], in_=vecs_d[:, :]), "misc", writes=[vecb])
        P.dma("sp", lambda e: e.dma_start(out=bada_t[:, :], in_=bada_d[:, :]), "misc", writes=[badab])
        P.dma("sp", lambda e: e.dma_start(out=ccol_t[:, :], in_=c_col[:, :]), "misc", writes=[ccolb])

        P.emit("act", lambda e: e.activation(out=cact_t[:, :], in_=ccol_t[:, :], func=AF.Silu),
               reads=[ccolb], writes=[cactb])
        ada_def = []
        ada_cnt = [0]

        def ada_flush():
            for (li, n, r) in ada_def:
                bi = alloc_bank()
                bk = banks_t[bi]
                for j in range(4):
                    P.emit("pe", (lambda e, bk=bk, j=j, r=r: e.matmul(
                        bk[:, j:j + 1], lhsT=row_t[0:1, r * 512 + j * 128:r * 512 + (j + 1) * 128],
                        rhs=ones_t[0:1, 0:1], start=True, stop=True)),
                        reads=[rowb[r], constb], writes=[bankb[bi]], inc=(j == 3))
                c0 = li * NMOD * NK + 4 * n
                P.emit("dve", (lambda e, bk=bk, c0=c0: e.tensor_tensor(
                    out=mods_t[:, c0:c0 + 4], in0=bk[:, 0:4], in1=bada_t[:, c0:c0 + 4], op=ALU.add)),
                    reads=[bankb[bi], badab], writes=[modsb[li]])
            del ada_def[:]

        def ada_rows(li, n):
            st, sbuf_ = ws_get("ada")
            v = st[:, 0:8192].rearrange("p (k c) -> p k c", k=NK, c=512)
            bi = alloc_bank()
            bk = banks_t[bi]
            for k in range(NK):
                P.emit("pe", (lambda e, bk=bk, v=v, k=k: e.matmul(
                    bk[0:1, 0:512], lhsT=cact_t[:, k:k + 1], rhs=v[:, k, :],
                    start=(k == 0), stop=(k == NK - 1))),
                    reads=[sbuf_, cactb], writes=[bankb[bi]], inc=(k == NK - 1))
            r = 0
            ada_cnt[0] += 1
            P.emit("act", (lambda e, bk=bk, r=r: e.activation(
                out=row_t[0:1, r * 512:(r + 1) * 512], in_=bk[0:1, 0:512], func=AF.Identity)),
                reads=[bankb[bi]], writes=[rowb[r]])
            ada_def.append((li, n, r))
            ada_done.add((li, n))

        def ada_derived(i, sub):
            ada_flush()
            assert all((i, n) in ada_done for n in range(NUP * (sub + 1))), (i, sub)
            vn, msc = ((V_NF1(i), 1), (V_NM(i), 4), (V_NF2(i), 7))[sub]
            c_sc = (i * NMOD + msc) * NK
            c_o = (i * 6 + sub) * NK
            P.emit("dve", (lambda e, c_sc=c_sc, c_o=c_o, vn=vn: e.scalar_tensor_tensor(
                out=der_t[:, c_o:c_o + NK], in0=mods_t[:, c_sc:c_sc + NK], scalar=1.0,
                in1=vec_t[:, vn * NK:(vn + 1) * NK], op0=ALU.add, op1=ALU.mult)),
                reads=[modsb[i], vecb], writes=[derb[i]])
            c_g = (i * NMOD + 3 * sub + 2) * NK
            c_o = (i * 6 + 3 + sub) * NK
            if sub != 1:
                P.emit("dve", (lambda e, c_g=c_g, c_o=c_o: e.tensor_scalar(
                    out=der_t[:, c_o:c_o + NK], in0=mods_t[:, c_g:c_g + NK], scalar1=0.5, scalar2=None,
                    op0=ALU.mult)), reads=[modsb[i]], writes=[derb[i]])
            elif i % 2 == 0:
                P.emit("dve", (lambda e, c_g=c_g, c_o=c_o: e.tensor_copy(
                    out=der_t[:, c_o:c_o + NK], in_=mods_t[:, c_g:c_g + NK])), reads=[modsb[i]], writes=[derb[i]])
            else:
                vp = V_PS(i // 2)
                P.emit("dve", (lambda e, c_g=c_g, c_o=c_o, vp=vp: e.tensor_tensor(
                    out=der_t[:, c_o:c_o + NK], in0=mods_t[:, c_g:c_g + NK],
                    in1=vec_t[:, vp * NK:(vp + 1) * NK], op=ALU.mult)), reads=[modsb[i], vecb], writes=[derb[i]])

        for n in range(NUP):
            ada_flush()
            ada_rows(0, n)
        ada_flush()

        def norm_stats(hook=None):
            bis = [alloc_bank() for _ in range(NSUB)]
            for k in range(NK):
                q = k % 3
                if k % 2 == 0:
                    P.emit("act", (lambda e, q=q, k=k: e.activation(
                        out=tmp[q][:, PAD:PAD + TT], in_=x_t[:, k, :], func=AF.Square)),
                        reads=list(xb[k]), writes=[tmpb[q]])
                else:
                    P.emit("dve", (lambda e, q=q, k=k: e.tensor_tensor(
                        out=tmp[q][:, PAD:PAD + TT], in0=x_t[:, k, :], in1=x_t[:, k, :], op=ALU.mult)),
                        reads=list(xb[k]), writes=[tmpb[q]])
                if hook is not None:
                    hook(k)
                for s in range(NSUB):
                    o, n = SUBS[s]
                    P.emit("pe", (lambda e, bk=banks_t[bis[s]], q=q, k=k, o=o, n=n: e.matmul(
                        bk[:, 0:n], lhsT=ones_t[:, :], rhs=tmp[q][:, PAD + o:PAD + o + n],
                        start=(k == 0), stop=(k == NK - 1))),
                        reads=[tmpb[q], constb], writes=[bankb[bis[s]]], inc=(s == NSUB - 1))
            for s in range(NSUB):
                o, n = SUBS[s]
                q = s % 2
                P.emit("act", (lambda e, bk=banks_t[bis[s]], q=q, n=n: e.activation(
                    out=sg_t[q][:, 0:n], in_=bk[:, 0:n], func=AF.Sqrt, scale=1.0 / D, bias=eps_t[:, 0:1])),
                    reads=[bankb[bis[s]], constb], writes=[sgb[q]])
                P.emit("dve", (lambda e, q=q, o=o, n=n: e.reciprocal(out=rstd_t[:, o:o + n], in_=sg_t[q][:, 0:n])),
                       reads=[sgb[q]], writes=[rstdb[s]])

        def h_prime(i, sub, k):
            xin = x_t[:, k, :].rearrange("p (s n) -> p s n", s=NSUB)
            if k % 2 == 0:
                P.emit("dve", (lambda e, k=k, xin=xin: e.tensor_scalar(
                    out=h_t[:, k, :, 0:SN], in0=xin, scalar1=dcol(i, sub, k), scalar2=None, op0=ALU.mult)),
                    reads=list(xb[k]) + [derb[i]], writes=list(hb[k]))
            else:
                P.emit("act", (lambda e, k=k, xin=xin: e.activation(
                    out=h_t[:, k, :, 0:SN], in_=xin, func=AF.Identity, scale=dcol(i, sub, k))),
                    reads=list(xb[k]) + [derb[i]], writes=list(hb[k]))

        def h_shift(i, sub):
            c_sh = (i * NMOD + (0, 3, 6)[sub]) * NK
            for s in range(NSUB):
                P.emit("dve", (lambda e, s=s: e.tensor_copy(
                    out=h_t[:, :, s, SN], in_=mods_t[:, c_sh:c_sh + NK])),
                    reads=[modsb[i]], writes=[hb[k][s] for k in range(NK)])

        def out_proj(slot_ap, slot_buf, nf, rhs_fn, rhs_bufs_fn, dcs, gate_fn, col_fn, li):
            for dc in dcs:
                for s in range(NSUB):
                    o, n = SUBS[s]
                    bi = alloc_bank()
                    bk = banks_t[bi]
                    for f in range(nf):
                        P.emit("pe", (lambda e, bk=bk, f=f, dc=dc, s=s, n=n: e.matmul(
                            bk[:, 0:n], lhsT=col_fn(slot_ap, f, dc), rhs=rhs_fn(f, s),
                            start=(f == 0), stop=(f == nf - 1))),
                            reads=[slot_buf] + rhs_bufs_fn(f, s), writes=[bankb[bi]], inc=(f == nf - 1))
                    P.emit("dve", (lambda e, bk=bk, dc=dc, o=o, n=n: e.scalar_tensor_tensor(
                        out=x_t[:, dc, o:o + n], in0=bk[:, 0:n], scalar=gate_fn(dc), in1=x_t[:, dc, o:o + n],
                        op0=ALU.mult, op1=ALU.add)),
                        reads=[bankb[bi], derb[li], xb[dc][s]], writes=[xb[dc][s]])

        def ffn(i, which, tsh):
            gsub = 3 if which == 0 else 5

            def phase_b(g):
                st, sbf = ws_get("ffn_out")
                v = st[:, 0:8192].rearrange("p (f c) -> p f c", f=G, c=D)
                gb = g % 2
                out_proj(v, sbf, G,
                         lambda f, s: a_t[:, gb, f, SUBS[s][0]:SUBS[s][0] + SN],
                         lambda f, s: [ab[gb][f][s]],
                         range(NK),
                         lambda dc: dcol(i, gsub, dc),
                         lambda vv, f, dc: vv[:, f, dc * 128:(dc + 1) * 128], i)

            unit = 0
            for g in range(NG):
                gb = g % 2
                for half in range(2):
                    st, sbf = ws_get("ffn_in")
                    v = st[:, 0:8192].rearrange("p (k t c) -> p k t c", k=NK, t=2, c=256)
                    for jj in range(2):
                        fl = 2 * half + jj
                        for s in range(NSUB):
                            o, n = SUBS[s]
                            bg = alloc_bank()
                            bu = alloc_bank()
                            for t, bi in ((0, bg), (1, bu)):
                                bk = banks_t[bi]
                                for k in range(NK):
                                    P.emit("pe", (lambda e, bk=bk, v=v, k=k, t=t, jj=jj, s=s: e.matmul(
                                        bk[:, 0:NH], lhsT=v[:, k, t, jj * 128:(jj + 1) * 128],
                                        rhs=h_t[:, k, s, 0:NH], start=(k == 0), stop=(k == NK - 1))),
                                        reads=[sbf, hb[k][s]], writes=[bankb[bi]], inc=(k == NK - 1))
                            q = unit % 2
                            pb = (unit // NSUB) % 2
                            unit += 1
                            qa, qb = 2 * q, 2 * q + 1
                            if s == 0:
                                P.emit("dve", (lambda e, bg=bg, pb=pb: e.tensor_copy(
                                    out=bias_t[:, 2 * pb:2 * pb + 1], in_=banks_t[bg][:, SN:SN + 1])),
                                    reads=[bankb[bg]], writes=[biasb[pb]])
                            P.emit("dve", (lambda e, qa=qa, bg=bg, o=o: e.tensor_tensor(
                                out=sg_t[qa][:, 0:SN], in0=banks_t[bg][:, 0:SN], in1=rstd_t[:, o:o + SN], op=ALU.mult)),
                                reads=[bankb[bg], rstdb[s]], writes=[sgb[qa]])
                            P.emit("act", (lambda e, qa=qa, pb=pb: e.activation(
                                out=sg_t[qa][:, 0:SN], in_=sg_t[qa][:, 0:SN], func=AF.Silu,
                                bias=bias_t[:, 2 * pb:2 * pb + 1])),
                                reads=[sgb[qa], biasb[pb]], writes=[sgb[qa]])
                            P.emit("dve", (lambda e, qb=qb, bu=bu, o=o: e.tensor_tensor(
                                out=sg_t[qb][:, 0:SN], in0=banks_t[bu][:, 0:SN], in1=rstd_t[:, o:o + SN], op=ALU.mult)),
                                reads=[bankb[bu], rstdb[s]], writes=[sgb[qb]])
                            P.emit("dve", (lambda e, qa=qa, qb=qb, bu=bu, gb=gb, fl=fl, o=o: e.scalar_tensor_tensor(
                                out=a_t[:, gb, fl, o:o + SN], in0=sg_t[qb][:, 0:SN], scalar=banks_t[bu][:, SN:SN + 1],
                                in1=sg_t[qa][:, 0:SN], op0=ALU.add, op1=ALU.mult)),
                                reads=[sgb[qa], sgb[qb], bankb[bu]], writes=[ab[gb][fl][s]])
                    if tsh == 0 and ("f", i, which, g, half) in ADA_AT:
                        ada_flush()
                        ada_rows(*ADA_AT[("f", i, which, g, half)])
                if g >= 1:
                    phase_b(g - 1)
            phase_b(NG - 1)
            if ada_def:
                ada_flush()

        def conv_mixer(i, tsh):
            j = i // 2
            unit = 0

            def phase_b(q4):
                st, sbf = ws_get("conv_out")
                v = st[:, 0:8192].rearrange("p (f c) -> p f c", f=G, c=D)
                gb = q4 % 2
                out_proj(v, sbf, G,
                         lambda f, s: a_t[:, gb, f, SUBS[s][0]:SUBS[s][0] + SN],
                         lambda f, s: [ab[gb][f][s]],
                         range(NK),
                         lambda dc: dcol(i, 4, dc),
                         lambda vv, f, dc: vv[:, f, dc * 128:(dc + 1) * 128], i)

            for q4 in range(4):
                gb = q4 % 2
                for jl in range(4):
                    jj = 4 * q4 + jl
                    st, sbf = ws_get("conv_in")
                    v = st[:, 0:6144].rearrange("p (k t c) -> p k t c", k=NK, t=3, c=128)
                    for s in range(NSUB):
                        o, n = SUBS[s]
                        bis = [alloc_bank() for _ in range(3)]
                        for t in range(3):
                            bk = banks_t[bis[t]]
                            for k in range(NK):
                                P.emit("pe", (lambda e, bk=bk, v=v, k=k, t=t, s=s: e.matmul(
                                    bk[:, 0:NH], lhsT=v[:, k, t, :], rhs=h_t[:, k, s, 0:NH],
                                    start=(k == 0), stop=(k == NK - 1))),
                                    reads=[sbf, hb[k][s]], writes=[bankb[bis[t]]], inc=(k == NK - 1))
                        q = unit % 2
                        pb = (unit // NSUB) % 2
                        unit += 1
                        qa, qb = 2 * q, 2 * q + 1
                        bB, bC, bV = bis
                        if s == 0:
                            P.emit("dve", (lambda e, bC=bC, pb=pb: e.tensor_copy(
                                out=bias_t[:, 2 * pb:2 * pb + 1], in_=banks_t[bC][:, SN:SN + 1])),
                                reads=[bankb[bC]], writes=[biasb[pb]])
                            P.emit("dve", (lambda e, bB=bB, pb=pb: e.tensor_copy(
                                out=bias_t[:, 2 * pb + 1:2 * pb + 2], in_=banks_t[bB][:, SN:SN + 1])),
                                reads=[bankb[bB]], writes=[biasb[pb]])
                        P.emit("dve", (lambda e, qa=qa, bC=bC, o=o: e.tensor_tensor(
                            out=sg_t[qa][:, 0:SN], in0=banks_t[bC][:, 0:SN], in1=rstd_t[:, o:o + SN], op=ALU.mult)),
                            reads=[bankb[bC], rstdb[s]], writes=[sgb[qa]])
                        P.emit("act", (lambda e, qa=qa, pb=pb: e.activation(
                            out=sg_t[qa][:, 0:SN], in_=sg_t[qa][:, 0:SN], func=AF.Identity,
                            bias=bias_t[:, 2 * pb:2 * pb + 1])),
                            reads=[sgb[qa], biasb[pb]], writes=[sgb[qa]])
                        P.emit("dve", (lambda e, qb=qb, bV=bV, o=o: e.tensor_tensor(
                            out=sg_t[qb][:, 0:SN], in0=banks_t[bV][:, 0:SN], in1=rstd_t[:, o:o + SN], op=ALU.mult)),
                            reads=[bankb[bV], rstdb[s]], writes=[sgb[qb]])
                        P.emit("dve", (lambda e, qa=qa, qb=qb, bV=bV, o=o: e.scalar_tensor_tensor(
                            out=tmp[0][:, PAD + o:PAD + o + SN], in0=sg_t[qb][:, 0:SN], scalar=banks_t[bV][:, SN:SN + 1],
                            in1=sg_t[qa][:, 0:SN], op0=ALU.add, op1=ALU.mult)),
                            reads=[sgb[qa], sgb[qb], bankb[bV]], writes=[tmpb[0]])
                        if s == 0:
                            P.emit("dve", (lambda e: e.tensor_tensor(
                                out=tmp[0][:, PAD:PAD + ICW], in0=tmp[0][:, PAD:PAD + ICW], in1=mask_t[:, 0:ICW],
                                op=ALU.mult)), reads=[tmpb[0], maskb], writes=[tmpb[0]])
                        P.emit("dve", (lambda e, bB=bB, o=o: e.tensor_tensor(
                            out=tmp[2][:, PAD + o:PAD + o + SN], in0=banks_t[bB][:, 0:SN], in1=rstd_t[:, o:o + SN],
                            op=ALU.mult)), reads=[bankb[bB], rstdb[s]], writes=[tmpb[2]])
                        P.emit("act", (lambda e, pb=pb, o=o: e.activation(
                            out=tmp[2][:, PAD + o:PAD + o + SN], in_=tmp[2][:, PAD + o:PAD + o + SN], func=AF.Identity,
                            bias=bias_t[:, 2 * pb + 1:2 * pb + 2])),
                            reads=[tmpb[2], biasb[pb]], writes=[tmpb[2]])
                    u = tmp[0]
                    y = tmp[1]
                    P.emit("dve", (lambda e, jj=jj: e.tensor_scalar(
                        out=y[:, PAD:PAD + TT], in0=u[:, PAD:PAD + TT], scalar1=vcol(V_CW(j, 2), jj), scalar2=None,
                        op0=ALU.mult)), reads=[tmpb[0], vecb], writes=[tmpb[1]])
                    for kk in (1, 2):
                        P.emit("dve", (lambda e, jj=jj, kk=kk: e.scalar_tensor_tensor(
                            out=y[:, PAD:PAD + TT], in0=u[:, PAD - kk:PAD - kk + TT], scalar=vcol(V_CW(j, 2 - kk), jj),
                            in1=y[:, PAD:PAD + TT], op0=ALU.mult, op1=ALU.add)),
                            reads=[tmpb[0], tmpb[1], vecb], writes=[tmpb[1]])
                    P.emit("dve", (lambda e, gb=gb, jl=jl: e.tensor_tensor(
                        out=a_t[:, gb, jl, :], in0=y[:, PAD:PAD + TT], in1=tmp[2][:, PAD:PAD + TT], op=ALU.mult)),
                        reads=[tmpb[1], tmpb[2]], writes=list(ab[gb][jl]))
                    if tsh == 0 and ("c", i, jj) in ADA_AT:
                        ada_flush()
                        ada_rows(*ADA_AT[("c", i, jj)])
                if q4 >= 1:
                    phase_b(q4 - 1)
            phase_b(3)

        def pool_mixer(i, tsh):
            j = i // 2
            all_ab = [b for gbl in ab for fl_ in gbl for b in fl_]
            bufsets = ((tmp[0], tmp[1], tmp[2], tmpb[0], tmpb[1], tmpb[2]),
                       (tmpB[0], tmpB[1], tmpB[2], tmpBb[0], tmpBb[1], tmpBb[2]))
            for q in range(3):
                P.emit("dve", (lambda e, q=q: e.memset(tmpB[q][:, 0:PAD], 0.0)), writes=all_ab + [tmpBb[q]])

            def chain(k, c):
                g = k // 4
                w = POOL_W[g]
                hp, p0, p1, hpb, p0b, p1b = bufsets[c]
                ta = ta_t[:, c * ICW:(c + 1) * ICW]
                P.emit("dve", (lambda e: e.tensor_tensor(
                    out=hp[:, PAD:PAD + TT], in0=x_t[:, k, :], in1=rstd_t[:, :], op=ALU.mult)),
                    reads=list(xb[k]) + rstdb, writes=[hpb])
                yield
                P.emit("dve", (lambda e: e.tensor_tensor(
                    out=hp[:, PAD:PAD + ICW], in0=hp[:, PAD:PAD + ICW], in1=mask_t[:, 0:ICW], op=ALU.mult)),
                    reads=[hpb, maskb], writes=[hpb])
                yield
                cur, curb = hp, hpb
                sh = 1
                step = 0
                while sh < w:
                    nxt, nxtb = (p0, p0b) if step % 2 == 0 else (p1, p1b)
                    P.emit("dve", (lambda e, cur=cur, nxt=nxt, sh=sh: e.tensor_tensor(
                        out=nxt[:, PAD:PAD + TT], in0=cur[:, PAD:PAD + TT], in1=cur[:, PAD - sh:PAD - sh + TT],
                        op=ALU.add)), reads=[curb], writes=[nxtb])
                    yield
                    cur, curb = nxt, nxtb
                    sh *= 2
                    step += 1
                P.emit("dve", (lambda e, cur=cur: e.tensor_tensor(
                    out=ta, in0=cur[:, PAD:PAD + ICW], in1=inv_t[:, g * ICW:(g + 1) * ICW], op=ALU.mult)),
                    reads=[curb, invb], writes=[tab[c]])
                yield
                P.emit("dve", (lambda e: e.tensor_tensor(
                    out=ta, in0=ta, in1=hp[:, PAD:PAD + ICW], op=ALU.subtract)),
                    reads=[tab[c], hpb], writes=[tab[c]])
                yield
                P.emit("dve", (lambda e, cur=cur: e.scalar_tensor_tensor(
                    out=hp[:, PAD:PAD + TT], in0=cur[:, PAD:PAD + TT], scalar=1.0 / w, in1=hp[:, PAD:PAD + TT],
                    op0=ALU.mult, op1=ALU.subtract)),
                    reads=[curb, hpb], writes=[hpb])
                yield
                P.emit("dve", (lambda e: e.tensor_copy(out=hp[:, PAD:PAD + ICW], in_=ta)),
                       reads=[tab[c], hpb], writes=[hpb])
                yield
                P.emit("act", (lambda e: e.activation(
                    out=h_t[:, k, :, 0:SN], in_=hp[:, PAD:PAD + TT].rearrange("p (s n) -> p s n", s=NSUB),
                    func=AF.Identity, scale=dcol(i, 1, k))),
                    reads=[hpb, derb[i]], writes=list(hb[k]))
                yield

            def run_pair(k0):
                gens = [chain(k0, 0), chain(k0 + 1, 1)]
                live = [True, True]
                while any(live):
                    for c in range(2):
                        if live[c]:
                            try:
                                next(gens[c])
                            except StopIteration:
                                live[c] = False

            for g in range(4):
                run_pair(4 * g)
                run_pair(4 * g + 2)
                st, sbf = ws_get("pool_w")
                v = st[:, 0:2048].rearrange("p (f c) -> p f c", f=4, c=512)
                out_proj(v, sbf, 4,
                         lambda f, s, g=g: h_t[:, g * 4 + f, s, 0:SN],
                         lambda f, s, g=g: [hb[g * 4 + f][s]],
                         range(g * 4, g * 4 + 4),
                         lambda dc: dcol(i, 4, dc),
                         lambda vv, f, dc: vv[:, f, (dc % 4) * 128:(dc % 4 + 1) * 128], i)
                if tsh == 0 and ("p", i, g) in ADA_AT:
                    ada_flush()
                    ada_rows(*ADA_AT[("p", i, g)])
            for q in range(3):
                P.emit("dve", (lambda e, q=q: e.memset(tmpB[q][:, 0:PAD], 0.0)), writes=all_ab + [tmpBb[q]])

        for t in range(NT):
            xv = x_in[t].rearrange("(k p) n -> p k n", p=128)
            for k4 in range(0, NK, 4):
                P.dma("sp", (lambda e, k4=k4, xv=xv: e.dma_start(out=x_t[:, k4:k4 + 4, :], in_=xv[:, k4:k4 + 4, :])),
                      "xld", writes=[b for k in range(k4, k4 + 4) for b in xb[k]])
            P.dma("sp", (lambda e, t=t: e.dma_start(out=mask_t[:, :], in_=maskd[t])), "xld", writes=[maskb])
            P.dma("sp", (lambda e, t=t: e.dma_start(out=inv_t[:, :], in_=invd[t])), "xld", writes=[invb])
            for i in range(n_layers):
                if t == 0:
                    ada_derived(i, 0)
                norm_stats(lambda k, i=i: h_prime(i, 0, k))
                h_shift(i, 0)
                ffn(i, 0, t)
                if t == 0:
                    ada_derived(i, 1)
                if i % 2 == 0:
                    norm_stats(lambda k, i=i: h_prime(i, 1, k))
                    h_shift(i, 1)
                    conv_mixer(i, t)
                else:
                    norm_stats()
                    pool_mixer(i, t)
                if t == 0:
                    ada_derived(i, 2)
                norm_stats(lambda k, i=i: h_prime(i, 2, k))
                h_shift(i, 2)
                ffn(i, 1, t)
            norm_stats()
            for k in range(NK):
                q = k % 2
                P.emit("dve", (lambda e, q=q, k=k: e.tensor_tensor(
                    out=tmp[q][:, PAD:PAD + TT], in0=x_t[:, k, :], in1=rstd_t[:, :], op=ALU.mult)),
                    reads=list(xb[k]) + rstdb, writes=[tmpb[q]])
                P.emit("act", (lambda e, q=q, k=k: e.activation(
                    out=x_t[:, k, :], in_=tmp[q][:, PAD:PAD + TT], func=AF.Identity, scale=vcol(V_FIN, k))),
                    reads=[tmpb[q], vecb], writes=list(xb[k]))
            ov = out_d[t].rearrange("(k p) n -> p k n", p=128)
            for k4 in range(0, NK, 4):
                P.dma("sp", (lambda e, k4=k4, ov=ov: e.dma_start(out=ov[:, k4:k4 + 4, :], in_=x_t[:, k4:k4 + 4, HALO:HALO + OWN])),
                      "st", reads=[b for k in range(k4, k4 + 4) for b in xb[k]], writes=[outb])
        assert ws_state["next"] == len(plan), (ws_state, len(plan))
        final = [("st", P.dma_cnt["st"])] + [(e, P.cnt[e]) for e in ("pe", "act", "dve") if P.cnt[e] > 0]
        for k in P.dma_cnt:
            final.append((k, P.dma_cnt[k]))
        P.wait_all("sp", final)

        def run(eng, e):
            for waits, fn, incinfo in P.streams[eng]:
                for (k, v) in waits:
                    e.wait_ge(sems[k], v)
                if fn is None:
                    continue
                ins = fn(e)
                if incinfo is not None:
                    ins.then_inc(sems[incinfo[0]], incinfo[1])

        with nc.Block() as block:
            @block.tensor
            def _(e):
                run("pe", e)

            @block.scalar
            def _(e):
                run("act", e)

            @block.vector
            def _(e):
                run("dve", e)

            @block.gpsimd
            def _(e):
                run("pool", e)

            @block.sync
            def _(e):
                run("sp", e)
    return nc, {e: len(P.streams[e]) for e in P.ENGS}


def _cols(v):
    return np.ascontiguousarray(np.asarray(v, np.float32).reshape(NK, 128).T)


def _prep_inputs(x, c, norm_ffn1, norm_mix, norm_ffn2, w_ada, b_ada, w_ffn1_in, w_ffn1_out,
                 w_ffn2_in, w_ffn2_out, conv_in, conv_w, conv_out, pool_w, pool_scale, final_norm):
    x = np.asarray(x, np.float32)
    vec_list = []
    for i in range(DEPTH):
        vec_list += [_cols(norm_ffn1[i]), _cols(norm_mix[i]), _cols(norm_ffn2[i])]
    vec_list.append(_cols(final_norm))
    for j in range(2):
        for kk in range(3):
            vec_list.append(_cols(np.asarray(conv_w)[j, kk]))
    for j in range(2):
        vec_list.append(_cols(np.asarray(pool_scale)[j]))
    vecs = np.ascontiguousarray(np.concatenate(vec_list, axis=1))
    b_ada = np.asarray(b_ada, np.float32)
    bada = np.ascontiguousarray(np.concatenate(
        [b_ada[i].reshape(NMOD * NK, 128).T for i in range(DEPTH)], axis=1))
    shared = {
        "vecs": vecs, "bada": bada,
        "w_ada": np.asarray(w_ada, np.float32),
        "w_ffn1_in": np.asarray(w_ffn1_in, np.float32), "w_ffn1_out": np.asarray(w_ffn1_out, np.float32),
        "w_ffn2_in": np.asarray(w_ffn2_in, np.float32), "w_ffn2_out": np.asarray(w_ffn2_out, np.float32),
        "conv_in": np.asarray(conv_in, np.float32), "conv_out": np.asarray(conv_out, np.float32),
        "pool_w": np.asarray(pool_w, np.float32),
    }
    in_maps = []
    for core in range(NCORES):
        b = core // 2
        half = core % 2
        xin = np.zeros((NT, D, TT), np.float32)
        mask = np.zeros((NT, 128, TT), ml_dtypes.bfloat16)
        inv = np.ones((NT, 128, 4, ICW), np.float32)
        for t in range(NT):
            start = (half * NT + t) * OWN - HALO
            lo = max(start, 0)
            xin[t][:, lo - start:TTR] = x[b, lo:start + TTR, :].T
            mask[t][:, lo - start:] = 1.0
            pos = start + np.arange(ICW)
            for g, w in enumerate(POOL_W):
                cnt = np.minimum(np.maximum(pos, 0) + 1, w).astype(np.float32)
                inv[t][:, g, :] = (1.0 / cnt)[None, :]
        m = dict(shared)
        m["x_in"] = xin
        m["maskd"] = mask
        m["invd"] = np.ascontiguousarray(inv.reshape(NT, 128, 4 * ICW))
        m["c_col"] = _cols(np.asarray(c, np.float32)[b])
        in_maps.append(m)
    return in_maps


_CACHE = {}


def kernel(**inputs):
    in_maps = _prep_inputs(**inputs)
    if "nc" not in _CACHE:
        _CACHE["nc"] = build_program(DEPTH)[0]
    nc = _CACHE["nc"]
    res = run_bass_kernel_spmd(nc, in_maps, core_ids=list(range(NCORES)))
    out = np.empty((BATCH, SEQ, D), np.float32)
    for core in range(NCORES):
        b = core // 2
        half = core % 2
        o = res.results[core]["out"]
        for t in range(NT):
            s0 = (half * NT + t) * OWN
            out[b, s0:s0 + OWN, :] = o[t].T
    return out
```

```python
import contextlib
import numpy as np
import ml_dtypes
import concourse.bass as bass
import concourse.mybir as mybir
from concourse.bass_utils import run_bass_kernel_spmd

F32 = mybir.dt.float32
BF16 = mybir.dt.bfloat16
AF = mybir.ActivationFunctionType
ALU = mybir.AluOpType

D = 2048
BATCH = 4
SEQ = 4096
DEPTH = 4
FF = 5632
NK = D // 128
NFC = FF // 128
G = 4
NG = NFC // G
NMOD = 9
EPS = 1e-6
POOL_W = (2, 4, 8, 16)
HALO = 34
NCORES = 8
NT = 2
OWN = (SEQ // 2) // NT
TTR = OWN + HALO
TT = TTR + 1
NSUB = -(-TT // 512)
_b = TT // NSUB
_r = TT % NSUB
SUBS = []
_o = 0
for _i in range(NSUB):
    _n = _b + (1 if _i < _r else 0)
    SUBS.append((_o, _n))
    _o += _n
assert all(n == SUBS[0][1] for _, n in SUBS)
SN = SUBS[0][1]
NH = SN + 1
PAD = 16
NSLOT = 3
SLOT_ELEMS = 8192
NVEC = 21
ICW = 64


class Buf:
    __slots__ = ("w", "r")

    def __init__(self):
        self.w = None
        self.r = []


class Prog:
    ENGS = ("pe", "act", "dve", "pool", "sp")

    def __init__(self):
        self.streams = {e: [] for e in self.ENGS}
        self.cnt = {e: 0 for e in self.ENGS}
        self.pend = {e: [] for e in self.ENGS}
        self.waited = {e: {} for e in self.ENGS}
        self.dma_cnt = {}

    def _deps(self, eng, reads, writes, extra):
        deps = {}

        def add(tok):
            if tok is None:
                return
            k, v = tok
            if v > deps.get(k, 0):
                deps[k] = v
        for b in reads:
            add(b.w)
        for b in writes:
            add(b.w)
            for t in b.r:
                add(t)
        for t in extra:
            add(t)
        waits = []
        wd = self.waited[eng]
        for k, v in deps.items():
            if k == eng and eng in ("pe", "sp", "pool"):
                continue
            if wd.get(k, 0) < v:
                wd[k] = v
                waits.append((k, v))
        return waits

    def _assign(self, tok, reads, writes):
        for b in reads:
            b.r.append(tok)
        for b in writes:
            b.w = tok
            b.r = []

    def emit(self, eng, fn, reads=(), writes=(), inc=True, extra=()):
        waits = self._deps(eng, reads, writes, extra)
        tok = None
        incinfo = None
        if inc:
            self.cnt[eng] += 1
            tok = (eng, self.cnt[eng])
            incinfo = (eng, 1)
            for (r, w) in self.pend[eng]:
                self._assign(tok, r, w)
            self.pend[eng] = []
            self._assign(tok, reads, writes)
        else:
            self.pend[eng].append((tuple(reads), tuple(writes)))
        self.streams[eng].append((waits, fn, incinfo))
        return tok

    def dma(self, queue, fn, semkey, reads=(), writes=()):
        return self.dma_group(queue, [fn], semkey, reads, writes)

    def dma_group(self, queue, fns, semkey, reads=(), writes=()):
        waits = self._deps(queue, reads, writes, ())
        for fn in fns:
            self.dma_cnt[semkey] = self.dma_cnt.get(semkey, 0) + 16
            self.streams[queue].append((waits, fn, (semkey, 16)))
            waits = []
        tok = (semkey, self.dma_cnt[semkey])
        self._assign(tok, reads, writes)
        return tok

    def wait_all(self, eng, toks):
        waits = []
        wd = self.waited[eng]
        for (k, v) in toks:
            if wd.get(k, 0) < v:
                wd[k] = v
                waits.append((k, v))
        self.streams[eng].append((waits, None, None))


def build_program(n_layers=DEPTH):
    nc = bass.Bass("TRN2", target_bir_lowering=False)
    P = Prog()

    def din(name, shape, dt=F32):
        return nc.dram_tensor(name, list(shape), dt, kind="ExternalInput").ap()

    x_in = din("x_in", [NT, D, TT])
    maskd = din("maskd", [NT, 128, TT], BF16)
    invd = din("invd", [NT, 128, 4 * ICW])
    c_col = din("c_col", [128, NK])
    vecs_d = din("vecs", [128, NVEC * NK])
    bada_d = din("bada", [128, DEPTH * NMOD * NK])
    w_ada = din("w_ada", [DEPTH, D, NMOD * D])
    w_f1i = din("w_ffn1_in", [DEPTH, D, 2 * FF])
    w_f1o = din("w_ffn1_out", [DEPTH, FF, D])
    w_f2i = din("w_ffn2_in", [DEPTH, D, 2 * FF])
    w_f2o = din("w_ffn2_out", [DEPTH, FF, D])
    conv_in = din("conv_in", [2, D, 3 * D])
    conv_out = din("conv_out", [2, D, D])
    pool_w = din("pool_w", [2, 4, 512, 512])
    out_d = nc.dram_tensor("out", [NT, D, OWN], F32, kind="ExternalOutput").ap()

    es = contextlib.ExitStack()
    with es:
        def sb(name, shape, dt=F32):
            return es.enter_context(nc.sbuf_tensor(name, list(shape), dt))

        x_t = sb("x_t", [128, NK, TT])
        h_t = sb("h_t", [128, NK, NSUB, NH], BF16)
        a_flat = sb("a_t", [128, 2 * G * TT], BF16)
        a_t = a_flat[:, :].rearrange("p (a g t) -> p a g t", a=2, g=G, t=TT)
        a_f32 = a_flat.bitcast(F32)
        tmpB = [a_f32[:, q * (PAD + TT):(q + 1) * (PAD + TT)] for q in range(3)]
        rstd_t = sb("rstd_t", [128, TT])
        tmp = [sb(f"tmp{i}", [128, PAD + TT]) for i in range(3)]
        sg_t = [sb(f"sg{i}", [128, 512]) for i in range(4)]
        bias_t = sb("bias_t", [128, 4])
        mask_t = sb("mask_t", [128, TT], BF16)
        inv_t = sb("inv_t", [128, 4 * ICW])
        ta_t = sb("ta_t", [128, 2 * ICW])
        vec_t = sb("vec_t", [128, NVEC * NK])
        bada_t = sb("bada_t", [128, DEPTH * NMOD * NK])
        mods_t = sb("mods_t", [128, DEPTH * NMOD * NK])
        der_t = sb("der_t", [128, DEPTH * 6 * NK])
        ccol_t = sb("ccol_t", [128, NK])
        cact_t = sb("cact_t", [128, NK], BF16)
        ones_t = sb("ones_t", [128, 128])
        eps_t = sb("eps_t", [128, 1])
        row_t = sb("row_t", [1, 512])
        slots_t = [sb(f"slot{i}", [128, SLOT_ELEMS], BF16) for i in range(NSLOT)]
        banks_t = [es.enter_context(nc.psum_tensor(f"bank{i}", [128, 512], F32)) for i in range(8)]

        sem_names = list(Prog.ENGS) + [f"slot{i}" for i in range(NSLOT)] + ["xld", "misc", "st"]
        sems = {k: es.enter_context(nc.semaphore("s_" + k)) for k in sem_names}

        xb = [[Buf() for _ in SUBS] for _ in range(NK)]
        hb = [[Buf() for _ in SUBS] for _ in range(NK)]
        ab = [[[Buf() for _ in SUBS] for _ in range(G)] for _ in range(2)]
        rstdb = [Buf() for _ in SUBS]
        tmpb = [Buf() for _ in range(3)]
        sgb = [Buf() for _ in range(4)]
        biasb = [Buf() for _ in range(2)]
        maskb = Buf()
        invb = Buf()
        tab = [Buf() for _ in range(2)]
        tmpBb = [Buf() for _ in range(3)]
        vecb = Buf()
        badab = Buf()
        modsb = [Buf() for _ in range(DEPTH)]
        derb = [Buf() for _ in range(DEPTH)]
        rowb = [Buf() for _ in range(1)]
        ccolb = Buf()
        cactb = Buf()
        constb = Buf()
        slotb = [Buf() for _ in range(NSLOT)]
        bankb = [Buf() for _ in range(8)]
        outb = Buf()

        bank_rr = [0]

        def alloc_bank():
            i = bank_rr[0]
            bank_rr[0] = (i + 1) % 8
            return i

        def wv_rows(w2d):
            return w2d.rearrange("(k p) n -> p k n", p=128)

        plan = []

        NADA = NMOD * D // 512
        NUP = 3 * D // 512
        pts = []
        for _i in range(n_layers):
            pts += [("f", _i, 0, g, hf) for g in range(NG) for hf in range(2)]
            if _i % 2 == 0:
                pts += [("c", _i, jj) for jj in range(NK)]
            else:
                pts += [("p", _i, g) for g in range(4)]
            pts += [("f", _i, 1, g, hf) for g in range(NG) for hf in range(2)]
        rest = [(li, n) for li in range(n_layers) for n in range(NADA)][NUP:]
        assert len(pts) >= len(rest)
        ADA_AT = dict(zip(pts, rest))
        ada_done = set()

        def plan_ada_tile(li, n):
            wav = wv_rows(w_ada[li])
            plan.append(("ada", [
                (lambda s: s[:, 0:8192].rearrange("p (k c) -> p k c", k=NK, c=512),
                 wav[:, :, n * 512:(n + 1) * 512])]))

        def plan_ffn(wi, wo, t, i, which):
            wiv = wv_rows(wi)
            wov = wv_rows(wo)
            for g in range(NG):
                for half in range(2):
                    c0 = (g * G + 2 * half) * 128
                    plan.append(("ffn_in", [
                        (lambda s: s[:, 0:8192].rearrange("p (k t c) -> p k t c", k=NK, t=2, c=256)[:, :, 0, :],
                         wiv[:, :, c0:c0 + 256]),
                        (lambda s: s[:, 0:8192].rearrange("p (k t c) -> p k t c", k=NK, t=2, c=256)[:, :, 1, :],
                         wiv[:, :, FF + c0:FF + c0 + 256]),
                    ]))
                    if t == 0 and ("f", i, which, g, half) in ADA_AT:
                        plan_ada_tile(*ADA_AT[("f", i, which, g, half)])
                if g >= 1:
                    plan.append(("ffn_out", [
                        (lambda s: s[:, 0:8192].rearrange("p (f c) -> p f c", f=G, c=D),
                         wov[:, (g - 1) * G:(g - 1) * G + G, :])]))
            plan.append(("ffn_out", [
                (lambda s: s[:, 0:8192].rearrange("p (f c) -> p f c", f=G, c=D),
                 wov[:, (NG - 1) * G:NG * G, :])]))

        def plan_conv(j, t, i):
            civ = wv_rows(conv_in[j])
            cov = wv_rows(conv_out[j])
            for q in range(4):
                for jj in range(4 * q, 4 * q + 4):
                    plan.append(("conv_in", [
                        ((lambda s, tt=tt: s[:, 0:6144].rearrange("p (k t c) -> p k t c", k=NK, t=3, c=128)[:, :, tt, :]),
                         civ[:, :, tt * D + jj * 128:tt * D + jj * 128 + 128]) for tt in range(3)]))
                    if t == 0 and ("c", i, jj) in ADA_AT:
                        plan_ada_tile(*ADA_AT[("c", i, jj)])
                if q >= 1:
                    plan.append(("conv_out", [
                        (lambda s: s[:, 0:8192].rearrange("p (f c) -> p f c", f=G, c=D),
                         cov[:, (q - 1) * 4:(q - 1) * 4 + 4, :])]))
            plan.append(("conv_out", [
                (lambda s: s[:, 0:8192].rearrange("p (f c) -> p f c", f=G, c=D),
                 cov[:, 12:16, :])]))

        def plan_pool(j, t, i):
            for g in range(4):
                plan.append(("pool_w", [
                    (lambda s: s[:, 0:2048].rearrange("p (f c) -> p f c", f=4, c=512),
                     pool_w[j, g].rearrange("(k p) n -> p k n", p=128))]))
                if t == 0 and ("p", i, g) in ADA_AT:
                    plan_ada_tile(*ADA_AT[("p", i, g)])

        for n in range(NUP):
            plan_ada_tile(0, n)
        for _t in range(NT):
            for i in range(n_layers):
                plan_ffn(w_f1i[i], w_f1o[i], _t, i, 0)
                if i % 2 == 0:
                    plan_conv(i // 2, _t, i)
                else:
                    plan_pool(i // 2, _t, i)
                plan_ffn(w_f2i[i], w_f2o[i], _t, i, 1)

        ws_state = {"issued": 0, "next": 0}

        def ws_get(kind):
            i = ws_state["next"]
            assert plan[i][0] == kind, (plan[i][0], kind, i)
            ws_state["next"] = i + 1
            while ws_state["issued"] < min(len(plan), i + NSLOT):
                j = ws_state["issued"]
                sl = j % NSLOT
                fns = []
                for (dstf, src) in plan[j][1]:
                    dst = dstf(slots_t[sl])
                    fns.append(lambda e, dst=dst, src=src: e.dma_start(out=dst, in_=src))
                P.dma_group("pool", fns, f"slot{sl}", writes=[slotb[sl]])
                ws_state["issued"] = j + 1
            sl = i % NSLOT
            return slots_t[sl], slotb[sl]

        def sl_(s):
            o, n = SUBS[s]
            return slice(o, o + n)

        def vcol(v, k):
            return vec_t[:, v * NK + k:v * NK + k + 1]

        def mcol(i, m, k):
            c = (i * NMOD + m) * NK + k
            return mods_t[:, c:c + 1]

        def dcol(i, m, k):
            c = (i * 6 + m) * NK + k
            return der_t[:, c:c + 1]

        V_NF1 = lambda i: 3 * i
        V_NM = lambda i: 3 * i + 1
        V_NF2 = lambda i: 3 * i + 2
        V_FIN = 12
        V_CW = lambda j, kk: 13 + 3 * j + kk
        V_PS = lambda j: 19 + j

        P.emit("dve", lambda e: e.memset(ones_t[:, :], 1.0), writes=[constb])
        P.emit("dve", lambda e: e.memset(eps_t[:, :], EPS), writes=[constb])
        for i in range(3):
            P.emit("dve", lambda e, i=i: e.memset(tmp[i][:, 0:PAD], 0.0), writes=[tmpb[i]])
        P.dma("sp", lambda e: e.dma_start(out=vec_t[:, :], in_=vecs_d[:, :]), "misc", writes=[vecb])
        P.dma("sp", lambda e: e.dma_start(out=bada_t[:, :], in_=bada_d[:, :]), "misc", writes=[badab])
        P.dma("sp", lambda e: e.dma_start(out=ccol_t[:, :], in_=c_col[:, :]), "misc", writes=[ccolb])

        P.emit("act", lambda e: e.activation(out=cact_t[:, :], in_=ccol_t[:, :], func=AF.Silu),
               reads=[ccolb], writes=[cactb])
        ada_def = []
        ada_cnt = [0]

        def ada_flush():
            for (li, n, r) in ada_def:
                bi = alloc_bank()
                bk = banks_t[bi]
                for j in range(4):
                    P.emit("pe", (lambda e, bk=bk, j=j, r=r: e.matmul(
                        bk[:, j:j + 1], lhsT=row_t[0:1, r * 512 + j * 128:r * 512 + (j + 1) * 128],
                        rhs=ones_t[0:1, 0:1], start=True, stop=True)),
                        reads=[rowb[r], constb], writes=[bankb[bi]], inc=(j == 3))
                c0 = li * NMOD * NK + 4 * n
                P.emit("dve", (lambda e, bk=bk, c0=c0: e.tensor_tensor(
                    out=mods_t[:, c0:c0 + 4], in0=bk[:, 0:4], in1=bada_t[:, c0:c0 + 4], op=ALU.add)),
                    reads=[bankb[bi], badab], writes=[modsb[li]])
            del ada_def[:]

        def ada_rows(li, n):
            st, sbuf_ = ws_get("ada")
            v = st[:, 0:8192].rearrange("p (k c) -> p k c", k=NK, c=512)
            bi = alloc_bank()
            bk = banks_t[bi]
            for k in range(NK):
                P.emit("pe", (lambda e, bk=bk, v=v, k=k: e.matmul(
                    bk[0:1, 0:512], lhsT=cact_t[:, k:k + 1], rhs=v[:, k, :],
                    start=(k == 0), stop=(k == NK - 1))),
                    reads=[sbuf_, cactb], writes=[bankb[bi]], inc=(k == NK - 1))
            r = 0
            ada_cnt[0] += 1
            P.emit("act", (lambda e, bk=bk, r=r: e.activation(
                out=row_t[0:1, r * 512:(r + 1) * 512], in_=bk[0:1, 0:512], func=AF.Identity)),
                reads=[bankb[bi]], writes=[rowb[r]])
            ada_def.append((li, n, r))
            ada_done.add((li, n))

        def ada_derived(i, sub):
            ada_flush()
            assert all((i, n) in ada_done for n in range(NUP * (sub + 1))), (i, sub)
            vn, msc = ((V_NF1(i), 1), (V_NM(i), 4), (V_NF2(i), 7))[sub]
            c_sc = (i * NMOD + msc) * NK
            c_o = (i * 6 + sub) * NK
            P.emit("dve", (lambda e, c_sc=c_sc, c_o=c_o, vn=vn: e.scalar_tensor_tensor(
                out=der_t[:, c_o:c_o + NK], in0=mods_t[:, c_sc:c_sc + NK], scalar=1.0,
                in1=vec_t[:, vn * NK:(vn + 1) * NK], op0=ALU.add, op1=ALU.mult)),
                reads=[modsb[i], vecb], writes=[derb[i]])
            c_g = (i * NMOD + 3 * sub + 2) * NK
            c_o = (i * 6 + 3 + sub) * NK
            if sub != 1:
                P.emit("dve", (lambda e, c_g=c_g, c_o=c_o: e.tensor_scalar(
                    out=der_t[:, c_o:c_o + NK], in0=mods_t[:, c_g:c_g + NK], scalar1=0.5, scalar2=None,
                    op0=ALU.mult)), reads=[modsb[i]], writes=[derb[i]])
            elif i % 2 == 0:
                P.emit("dve", (lambda e, c_g=c_g, c_o=c_o: e.tensor_copy(
                    out=der_t[:, c_o:c_o + NK], in_=mods_t[:, c_g:c_g + NK])), reads=[modsb[i]], writes=[derb[i]])
            else:
                vp = V_PS(i // 2)
                P.emit("dve", (lambda e, c_g=c_g, c_o=c_o, vp=vp: e.tensor_tensor(
                    out=der_t[:, c_o:c_o + NK], in0=mods_t[:, c_g:c_g + NK],
                    in1=vec_t[:, vp * NK:(vp + 1) * NK], op=ALU.mult)), reads=[modsb[i], vecb], writes=[derb[i]])

        for n in range(NUP):
            ada_flush()
            ada_rows(0, n)
        ada_flush()

        def norm_stats(hook=None):
            bis = [alloc_bank() for _ in range(NSUB)]
            for k in range(NK):
                q = k % 3
                if k % 2 == 0:
                    P.emit("act", (lambda e, q=q, k=k: e.activation(
                        out=tmp[q][:, PAD:PAD + TT], in_=x_t[:, k, :], func=AF.Square)),
                        reads=list(xb[k]), writes=[tmpb[q]])
                else:
                    P.emit("dve", (lambda e, q=q, k=k: e.tensor_tensor(
                        out=tmp[q][:, PAD:PAD + TT], in0=x_t[:, k, :], in1=x_t[:, k, :], op=ALU.mult)),
                        reads=list(xb[k]), writes=[tmpb[q]])
                if hook is not None:
                    hook(k)
                for s in range(NSUB):
                    o, n = SUBS[s]
                    P.emit("pe", (lambda e, bk=banks_t[bis[s]], q=q, k=k, o=o, n=n: e.matmul(
                        bk[:, 0:n], lhsT=ones_t[:, :], rhs=tmp[q][:, PAD + o:PAD + o + n],
                        start=(k == 0), stop=(k == NK - 1))),
                        reads=[tmpb[q], constb], writes=[bankb[bis[s]]], inc=(s == NSUB - 1))
            for s in range(NSUB):
                o, n = SUBS[s]
                q = s % 2
                P.emit("act", (lambda e, bk=banks_t[bis[s]], q=q, n=n: e.activation(
                    out=sg_t[q][:, 0:n], in_=bk[:, 0:n], func=AF.Sqrt, scale=1.0 / D, bias=eps_t[:, 0:1])),
                    reads=[bankb[bis[s]], constb], writes=[sgb[q]])
                P.emit("dve", (lambda e, q=q, o=o, n=n: e.reciprocal(out=rstd_t[:, o:o + n], in_=sg_t[q][:, 0:n])),
                       reads=[sgb[q]], writes=[rstdb[s]])

        def h_prime(i, sub, k):
            xin = x_t[:, k, :].rearrange("p (s n) -> p s n", s=NSUB)
            if k % 2 == 0:
                P.emit("dve", (lambda e, k=k, xin=xin: e.tensor_scalar(
                    out=h_t[:, k, :, 0:SN], in0=xin, scalar1=dcol(i, sub, k), scalar2=None, op0=ALU.mult)),
                    reads=list(xb[k]) + [derb[i]], writes=list(hb[k]))
            else:
                P.emit("act", (lambda e, k=k, xin=xin: e.activation(
                    out=h_t[:, k, :, 0:SN], in_=xin, func=AF.Identity, scale=dcol(i, sub, k))),
                    reads=list(xb[k]) + [derb[i]], writes=list(hb[k]))

        def h_shift(i, sub):
            c_sh = (i * NMOD + (0, 3, 6)[sub]) * NK
            for s in range(NSUB):
                P.emit("dve", (lambda e, s=s: e.tensor_copy(
                    out=h_t[:, :, s, SN], in_=mods_t[:, c_sh:c_sh + NK])),
                    reads=[modsb[i]], writes=[hb[k][s] for k in range(NK)])

        def out_proj(slot_ap, slot_buf, nf, rhs_fn, rhs_bufs_fn, dcs, gate_fn, col_fn, li):
            for dc in dcs:
                for s in range(NSUB):
                    o, n = SUBS[s]
                    bi = alloc_bank()
                    bk = banks_t[bi]
                    for f in range(nf):
                        P.emit("pe", (lambda e, bk=bk, f=f, dc=dc, s=s, n=n: e.matmul(
                            bk[:, 0:n], lhsT=col_fn(slot_ap, f, dc), rhs=rhs_fn(f, s),
                            start=(f == 0), stop=(f == nf - 1))),
                            reads=[slot_buf] + rhs_bufs_fn(f, s), writes=[bankb[bi]], inc=(f == nf - 1))
                    P.emit("dve", (lambda e, bk=bk, dc=dc, o=o, n=n: e.scalar_tensor_tensor(
                        out=x_t[:, dc, o:o + n], in0=bk[:, 0:n], scalar=gate_fn(dc), in1=x_t[:, dc, o:o + n],
                        op0=ALU.mult, op1=ALU.add)),
                        reads=[bankb[bi], derb[li], xb[dc][s]], writes=[xb[dc][s]])

        def ffn(i, which, tsh):
            gsub = 3 if which == 0 else 5

            def phase_b(g):
                st, sbf = ws_get("ffn_out")
                v = st[:, 0:8192].rearrange("p (f c) -> p f c", f=G, c=D)
                gb = g % 2
                out_proj(v, sbf, G,
                         lambda f, s: a_t[:, gb, f, SUBS[s][0]:SUBS[s][0] + SN],
                         lambda f, s: [ab[gb][f][s]],
                         range(NK),
                         lambda dc: dcol(i, gsub, dc),
                         lambda vv, f, dc: vv[:, f, dc * 128:(dc + 1) * 128], i)

            unit = 0
            for g in range(NG):
                gb = g % 2
                for half in range(2):
                    st, sbf = ws_get("ffn_in")
                    v = st[:, 0:8192].rearrange("p (k t c) -> p k t c", k=NK, t=2, c=256)
                    for jj in range(2):
                        fl = 2 * half + jj
                        for s in range(NSUB):
                            o, n = SUBS[s]
                            bg = alloc_bank()
                            bu = alloc_bank()
                            for t, bi in ((0, bg), (1, bu)):
                                bk = banks_t[bi]
                                for k in range(NK):
                                    P.emit("pe", (lambda e, bk=bk, v=v, k=k, t=t, jj=jj, s=s: e.matmul(
                                        bk[:, 0:NH], lhsT=v[:, k, t, jj * 128:(jj + 1) * 128],
                                        rhs=h_t[:, k, s, 0:NH], start=(k == 0), stop=(k == NK - 1))),
                                        reads=[sbf, hb[k][s]], writes=[bankb[bi]], inc=(k == NK - 1))
                            q = unit % 2
                            pb = (unit // NSUB) % 2
                            unit += 1
                            qa, qb = 2 * q, 2 * q + 1
                            if s == 0:
                                P.emit("dve", (lambda e, bg=bg, pb=pb: e.tensor_copy(
                                    out=bias_t[:, 2 * pb:2 * pb + 1], in_=banks_t[bg][:, SN:SN + 1])),
                                    reads=[bankb[bg]], writes=[biasb[pb]])
                            P.emit("dve", (lambda e, qa=qa, bg=bg, o=o: e.tensor_tensor(
                                out=sg_t[qa][:, 0:SN], in0=banks_t[bg][:, 0:SN], in1=rstd_t[:, o:o + SN], op=ALU.mult)),
                                reads=[bankb[bg], rstdb[s]], writes=[sgb[qa]])
                            P.emit("act", (lambda e, qa=qa, pb=pb: e.activation(
                                out=sg_t[qa][:, 0:SN], in_=sg_t[qa][:, 0:SN], func=AF.Silu,
                                bias=bias_t[:, 2 * pb:2 * pb + 1])),
                                reads=[sgb[qa], biasb[pb]], writes=[sgb[qa]])
                            P.emit("dve", (lambda e, qb=qb, bu=bu, o=o: e.tensor_tensor(
                                out=sg_t[qb][:, 0:SN], in0=banks_t[bu][:, 0:SN], in1=rstd_t[:, o:o + SN], op=ALU.mult)),
                                reads=[bankb[bu], rstdb[s]], writes=[sgb[qb]])
                            P.emit("dve", (lambda e, qa=qa, qb=qb, bu=bu, gb=gb, fl=fl, o=o: e.scalar_tensor_tensor(
                                out=a_t[:, gb, fl, o:o + SN], in0=sg_t[qb][:, 0:SN], scalar=banks_t[bu][:, SN:SN + 1],
                                in1=sg_t[qa][:, 0:SN], op0=ALU.add, op1=ALU.mult)),
                                reads=[sgb[qa], sgb[qb], bankb[bu]], writes=[ab[gb][fl][s]])
                    if tsh == 0 and ("f", i, which, g, half) in ADA_AT:
                        ada_flush()
                        ada_rows(*ADA_AT[("f", i, which, g, half)])
                if g >= 1:
                    phase_b(g - 1)
            phase_b(NG - 1)
            if ada_def:
                ada_flush()

        def conv_mixer(i, tsh):
            j = i // 2
            unit = 0

            def phase_b(q4):
                st, sbf = ws_get("conv_out")
                v = st[:, 0:8192].rearrange("p (f c) -> p f c", f=G, c=D)
                gb = q4 % 2
                out_proj(v, sbf, G,
                         lambda f, s: a_t[:, gb, f, SUBS[s][0]:SUBS[s][0] + SN],
                         lambda f, s: [ab[gb][f][s]],
                         range(NK),
                         lambda dc: dcol(i, 4, dc),
                         lambda vv, f, dc: vv[:, f, dc * 128:(dc + 1) * 128], i)

            for q4 in range(4):
                gb = q4 % 2
                for jl in range(4):
                    jj = 4 * q4 + jl
                    st, sbf = ws_get("conv_in")
                    v = st[:, 0:6144].rearrange("p (k t c) -> p k t c", k=NK, t=3, c=128)
                    for s in range(NSUB):
                        o, n = SUBS[s]
                        bis = [alloc_bank() for _ in range(3)]
                        for t in range(3):
                            bk = banks_t[bis[t]]
                            for k in range(NK):
                                P.emit("pe", (lambda e, bk=bk, v=v, k=k, t=t, s=s: e.matmul(
                                    bk[:, 0:NH], lhsT=v[:, k, t, :], rhs=h_t[:, k, s, 0:NH],
                                    start=(k == 0), stop=(k == NK - 1))),
                                    reads=[sbf, hb[k][s]], writes=[bankb[bis[t]]], inc=(k == NK - 1))
                        q = unit % 2
                        pb = (unit // NSUB) % 2
                        unit += 1
                        qa, qb = 2 * q, 2 * q + 1
                        bB, bC, bV = bis
                        if s == 0:
                            P.emit("dve", (lambda e, bC=bC, pb=pb: e.tensor_copy(
                                out=bias_t[:, 2 * pb:2 * pb + 1], in_=banks_t[bC][:, SN:SN + 1])),
                                reads=[bankb[bC]], writes=[biasb[pb]])
                            P.emit("dve", (lambda e, bB=bB, pb=pb: e.tensor_copy(
                                out=bias_t[:, 2 * pb + 1:2 * pb + 2], in_=banks_t[bB][:, SN:SN + 1])),
                                reads=[bankb[bB]], writes=[biasb[pb]])
                        P.emit("dve", (lambda e, qa=qa, bC=bC, o=o: e.tensor_tensor(
                            out=sg_t[qa][:, 0:SN], in0=banks_t[bC][:, 0:SN], in1=rstd_t[:, o:o + SN], op=ALU.mult)),
                            reads=[bankb[bC], rstdb[s]], writes=[sgb[qa]])
                        P.emit("act", (lambda e, qa=qa, pb=pb: e.activation(
                            out=sg_t[qa][:, 0:SN], in_=sg_t[qa][:, 0:SN], func=AF.Identity,
                            bias=bias_t[:, 2 * pb:2 * pb + 1])),
                            reads=[sgb[qa], biasb[pb]], writes=[sgb[qa]])
                        P.emit("dve", (lambda e, qb=qb, bV=bV, o=o: e.tensor_tensor(
                            out=sg_t[qb][:, 0:SN], in0=banks_t[bV][:, 0:SN], in1=rstd_t[:, o:o + SN], op=ALU.mult)),
                            reads=[bankb[bV], rstdb[s]], writes=[sgb[qb]])
                        P.emit("dve", (lambda e, qa=qa, qb=qb, bV=bV, o=o: e.scalar_tensor_tensor(
                            out=tmp[0][:, PAD + o:PAD + o + SN], in0=sg_t[qb][:, 0:SN], scalar=banks_t[bV][:, SN:SN + 1],
                            in1=sg_t[qa][:, 0:SN], op0=ALU.add, op1=ALU.mult)),
                            reads=[sgb[qa], sgb[qb], bankb[bV]], writes=[tmpb[0]])
                        if s == 0:
                            P.emit("dve", (lambda e: e.tensor_tensor(
                                out=tmp[0][:, PAD:PAD + ICW], in0=tmp[0][:, PAD:PAD + ICW], in1=mask_t[:, 0:ICW],
                                op=ALU.mult)), reads=[tmpb[0], maskb], writes=[tmpb[0]])
                        P.emit("dve", (lambda e, bB=bB, o=o: e.tensor_tensor(
                            out=tmp[2][:, PAD + o:PAD + o + SN], in0=banks_t[bB][:, 0:SN], in1=rstd_t[:, o:o + SN],
                            op=ALU.mult)), reads=[bankb[bB], rstdb[s]], writes=[tmpb[2]])
                        P.emit("act", (lambda e, pb=pb, o=o: e.activation(
                            out=tmp[2][:, PAD + o:PAD + o + SN], in_=tmp[2][:, PAD + o:PAD + o + SN], func=AF.Identity,
                            bias=bias_t[:, 2 * pb + 1:2 * pb + 2])),
                            reads=[tmpb[2], biasb[pb]], writes=[tmpb[2]])
                    u = tmp[0]
                    y = tmp[1]
                    P.emit("dve", (lambda e, jj=jj: e.tensor_scalar(
                        out=y[:, PAD:PAD + TT], in0=u[:, PAD:PAD + TT], scalar1=vcol(V_CW(j, 2), jj), scalar2=None,
                        op0=ALU.mult)), reads=[tmpb[0], vecb], writes=[tmpb[1]])
                    for kk in (1, 2):
                        P.emit("dve", (lambda e, jj=jj, kk=kk: e.scalar_tensor_tensor(
                            out=y[:, PAD:PAD + TT], in0=u[:, PAD - kk:PAD - kk + TT], scalar=vcol(V_CW(j, 2 - kk), jj),
                            in1=y[:, PAD:PAD + TT], op0=ALU.mult, op1=ALU.add)),
                            reads=[tmpb[0], tmpb[1], vecb], writes=[tmpb[1]])
                    P.emit("dve", (lambda e, gb=gb, jl=jl: e.tensor_tensor(
                        out=a_t[:, gb, jl, :], in0=y[:, PAD:PAD + TT], in1=tmp[2][:, PAD:PAD + TT], op=ALU.mult)),
                        reads=[tmpb[1], tmpb[2]], writes=list(ab[gb][jl]))
                    if tsh == 0 and ("c", i, jj) in ADA_AT:
                        ada_flush()
                        ada_rows(*ADA_AT[("c", i, jj)])
                if q4 >= 1:
                    phase_b(q4 - 1)
            phase_b(3)

        def pool_mixer(i, tsh):
            j = i // 2
            all_ab = [b for gbl in ab for fl_ in gbl for b in fl_]
            bufsets = ((tmp[0], tmp[1], tmp[2], tmpb[0], tmpb[1], tmpb[2]),
                       (tmpB[0], tmpB[1], tmpB[2], tmpBb[0], tmpBb[1], tmpBb[2]))
            for q in range(3):
                P.emit("dve", (lambda e, q=q: e.memset(tmpB[q][:, 0:PAD], 0.0)), writes=all_ab + [tmpBb[q]])

            def chain(k, c):
                g = k // 4
                w = POOL_W[g]
                hp, p0, p1, hpb, p0b, p1b = bufsets[c]
                ta = ta_t[:, c * ICW:(c + 1) * ICW]
                P.emit("dve", (lambda e: e.tensor_tensor(
                    out=hp[:, PAD:PAD + TT], in0=x_t[:, k, :], in1=rstd_t[:, :], op=ALU.mult)),
                    reads=list(xb[k]) + rstdb, writes=[hpb])
                yield
                P.emit("dve", (lambda e: e.tensor_tensor(
                    out=hp[:, PAD:PAD + ICW], in0=hp[:, PAD:PAD + ICW], in1=mask_t[:, 0:ICW], op=ALU.mult)),
                    reads=[hpb, maskb], writes=[hpb])
                yield
                cur, curb = hp, hpb
                sh = 1
                step = 0
                while sh < w:
                    nxt, nxtb = (p0, p0b) if step % 2 == 0 else (p1, p1b)
                    P.emit("dve", (lambda e, cur=cur, nxt=nxt, sh=sh: e.tensor_tensor(
                        out=nxt[:, PAD:PAD + TT], in0=cur[:, PAD:PAD + TT], in1=cur[:, PAD - sh:PAD - sh + TT],
                        op=ALU.add)), reads=[curb], writes=[nxtb])
                    yield
                    cur, curb = nxt, nxtb
                    sh *= 2
                    step += 1
                P.emit("dve", (lambda e, cur=cur: e.tensor_tensor(
                    out=ta, in0=cur[:, PAD:PAD + ICW], in1=inv_t[:, g * ICW:(g + 1) * ICW], op=ALU.mult)),
                    reads=[curb, invb], writes=[tab[c]])
                yield
                P.emit("dve", (lambda e: e.tensor_tensor(
                    out=ta, in0=ta, in1=hp[:, PAD:PAD + ICW], op=ALU.subtract)),
                    reads=[tab[c], hpb], writes=[tab[c]])
                yield
                P.emit("dve", (lambda e, cur=cur: e.scalar_tensor_tensor(
                    out=hp[:, PAD:PAD + TT], in0=cur[:, PAD:PAD + TT], scalar=1.0 / w, in1=hp[:, PAD:PAD + TT],
                    op0=ALU.mult, op1=ALU.subtract)),
                    reads=[curb, hpb], writes=[hpb])
                yield
                P.emit("dve", (lambda e: e.tensor_copy(out=hp[:, PAD:PAD + ICW], in_=ta)),
                       reads=[tab[c], hpb], writes=[hpb])
                yield
                P.emit("act", (lambda e: e.activation(
                    out=h_t[:, k, :, 0:SN], in_=hp[:, PAD:PAD + TT].rearrange("p (s n) -> p s n", s=NSUB),
                    func=AF.Identity, scale=dcol(i, 1, k))),
                    reads=[hpb, derb[i]], writes=list(hb[k]))
                yield

            def run_pair(k0):
                gens = [chain(k0, 0), chain(k0 + 1, 1)]
                live = [True, True]
                while any(live):
                    for c in range(2):
                        if live[c]:
                            try:
                                next(gens[c])
                            except StopIteration:
                                live[c] = False

            for g in range(4):
                run_pair(4 * g)
                run_pair(4 * g + 2)
                st, sbf = ws_get("pool_w")
                v = st[:, 0:2048].rearrange("p (f c) -> p f c", f=4, c=512)
                out_proj(v, sbf, 4,
                         lambda f, s, g=g: h_t[:, g * 4 + f, s, 0:SN],
                         lambda f, s, g=g: [hb[g * 4 + f][s]],
                         range(g * 4, g * 4 + 4),
                         lambda dc: dcol(i, 4, dc),
                         lambda vv, f, dc: vv[:, f, (dc % 4) * 128:(dc % 4 + 1) * 128], i)
                if tsh == 0 and ("p", i, g) in ADA_AT:
                    ada_flush()
                    ada_rows(*ADA_AT[("p", i, g)])
            for q in range(3):
                P.emit("dve", (lambda e, q=q: e.memset(tmpB[q][:, 0:PAD], 0.0)), writes=all_ab + [tmpBb[q]])

        for t in range(NT):
            xv = x_in[t].rearrange("(k p) n -> p k n", p=128)
            for k4 in range(0, NK, 4):
                P.dma("sp", (lambda e, k4=k4, xv=xv: e.dma_start(out=x_t[:, k4:k4 + 4, :], in_=xv[:, k4:k4 + 4, :])),
                      "xld", writes=[b for k in range(k4, k4 + 4) for b in xb[k]])
            P.dma("sp", (lambda e, t=t: e.dma_start(out=mask_t[:, :], in_=maskd[t])), "xld", writes=[maskb])
            P.dma("sp", (lambda e, t=t: e.dma_start(out=inv_t[:, :], in_=invd[t])), "xld", writes=[invb])
            for i in range(n_layers):
                if t == 0:
                    ada_derived(i, 0)
                h_shift(i, 0)
                norm_stats(lambda k, i=i: h_prime(i, 0, k))
                ffn(i, 0, t)
                if t == 0:
                    ada_derived(i, 1)
                if i % 2 == 0:
                    h_shift(i, 1)
                    norm_stats(lambda k, i=i: h_prime(i, 1, k))
                    conv_mixer(i, t)
                else:
                    norm_stats()
                    pool_mixer(i, t)
                if t == 0:
                    ada_derived(i, 2)
                h_shift(i, 2)
                norm_stats(lambda k, i=i: h_prime(i, 2, k))
                ffn(i, 1, t)
            norm_stats()
            for k in range(NK):
                q = k % 2
                P.emit("dve", (lambda e, q=q, k=k: e.tensor_tensor(
                    out=tmp[q][:, PAD:PAD + TT], in0=x_t[:, k, :], in1=rstd_t[:, :], op=ALU.mult)),
                    reads=list(xb[k]) + rstdb, writes=[tmpb[q]])
                P.emit("act", (lambda e, q=q, k=k: e.activation(
                    out=x_t[:, k, :], in_=tmp[q][:, PAD:PAD + TT], func=AF.Identity, scale=vcol(V_FIN, k))),
                    reads=[tmpb[q], vecb], writes=list(xb[k]))
            ov = out_d[t].rearrange("(k p) n -> p k n", p=128)
            for k4 in range(0, NK, 4):
                P.dma("sp", (lambda e, k4=k4, ov=ov: e.dma_start(out=ov[:, k4:k4 + 4, :], in_=x_t[:, k4:k4 + 4, HALO:HALO + OWN])),
                      "st", reads=[b for k in range(k4, k4 + 4) for b in xb[k]], writes=[outb])
        assert ws_state["next"] == len(plan), (ws_state, len(plan))
        final = [("st", P.dma_cnt["st"])] + [(e, P.cnt[e]) for e in ("pe", "act", "dve") if P.cnt[e] > 0]
        for k in P.dma_cnt:
            final.append((k, P.dma_cnt[k]))
        P.wait_all("sp", final)

        def run(eng, e):
            for waits, fn, incinfo in P.streams[eng]:
                for (k, v) in waits:
                    e.wait_ge(sems[k], v)
                if fn is None:
                    continue
                ins = fn(e)
                if incinfo is not None:
                    ins.then_inc(sems[incinfo[0]], incinfo[1])

        with nc.Block() as block:
            @block.tensor
            def _(e):
                run("pe", e)

            @block.scalar
            def _(e):
                run("act", e)

            @block.vector
            def _(e):
                run("dve", e)

            @block.gpsimd
            def _(e):
                run("pool", e)

            @block.sync
            def _(e):
                run("sp", e)
    return nc, {e: len(P.streams[e]) for e in P.ENGS}


def _cols(v):
    return np.ascontiguousarray(np.asarray(v, np.float32).reshape(NK, 128).T)


def _prep_inputs(x, c, norm_ffn1, norm_mix, norm_ffn2, w_ada, b_ada, w_ffn1_in, w_ffn1_out,
                 w_ffn2_in, w_ffn2_out, conv_in, conv_w, conv_out, pool_w, pool_scale, final_norm):
    x = np.asarray(x, np.float32)
    vec_list = []
    for i in range(DEPTH):
        vec_list += [_cols(norm_ffn1[i]), _cols(norm_mix[i]), _cols(norm_ffn2[i])]
    vec_list.append(_cols(final_norm))
    for j in range(2):
        for kk in range(3):
            vec_list.append(_cols(np.asarray(conv_w)[j, kk]))
    for j in range(2):
        vec_list.append(_cols(np.asarray(pool_scale)[j]))
    vecs = np.ascontiguousarray(np.concatenate(vec_list, axis=1))
    b_ada = np.asarray(b_ada, np.float32)
    bada = np.ascontiguousarray(np.concatenate(
        [b_ada[i].reshape(NMOD * NK, 128).T for i in range(DEPTH)], axis=1))
    shared = {
        "vecs": vecs, "bada": bada,
        "w_ada": np.asarray(w_ada, np.float32),
        "w_ffn1_in": np.asarray(w_ffn1_in, np.float32), "w_ffn1_out": np.asarray(w_ffn1_out, np.float32),
        "w_ffn2_in": np.asarray(w_ffn2_in, np.float32), "w_ffn2_out": np.asarray(w_ffn2_out, np.float32),
        "conv_in": np.asarray(conv_in, np.float32), "conv_out": np.asarray(conv_out, np.float32),
        "pool_w": np.asarray(pool_w, np.float32),
    }
    in_maps = []
    for core in range(NCORES):
        b = core // 2
        half = core % 2
        xin = np.zeros((NT, D, TT), np.float32)
        mask = np.zeros((NT, 128, TT), ml_dtypes.bfloat16)
        inv = np.ones((NT, 128, 4, ICW), np.float32)
        for t in range(NT):
            start = (half * NT + t) * OWN - HALO
            lo = max(start, 0)
            xin[t][:, lo - start:TTR] = x[b, lo:start + TTR, :].T
            mask[t][:, lo - start:] = 1.0
            pos = start + np.arange(ICW)
            for g, w in enumerate(POOL_W):
                cnt = np.minimum(np.maximum(pos, 0) + 1, w).astype(np.float32)
                inv[t][:, g, :] = (1.0 / cnt)[None, :]
        m = dict(shared)
        m["x_in"] = xin
        m["maskd"] = mask
        m["invd"] = np.ascontiguousarray(inv.reshape(NT, 128, 4 * ICW))
        m["c_col"] = _cols(np.asarray(c, np.float32)[b])
        in_maps.append(m)
    return in_maps


_CACHE = {}


def kernel(**inputs):
    in_maps = _prep_inputs(**inputs)
    if "nc" not in _CACHE:
        _CACHE["nc"] = build_program(DEPTH)[0]
    nc = _CACHE["nc"]
    res = run_bass_kernel_spmd(nc, in_maps, core_ids=list(range(NCORES)))
    out = np.empty((BATCH, SEQ, D), np.float32)
    for core in range(NCORES):
        b = core // 2
        half = core % 2
        o = res.results[core]["out"]
        for t in range(NT):
            s0 = (half * NT + t) * OWN
            out[b, s0:s0 + OWN, :] = o[t].T
    return out
```

```python
import contextlib
import numpy as np
import ml_dtypes
import concourse.bass as bass
import concourse.mybir as mybir
from concourse.bass_utils import run_bass_kernel_spmd

F32 = mybir.dt.float32
BF16 = mybir.dt.bfloat16
AF = mybir.ActivationFunctionType
ALU = mybir.AluOpType

D = 2048
BATCH = 4
SEQ = 4096
DEPTH = 4
FF = 5632
NK = D // 128
NFC = FF // 128
G = 4
NG = NFC // G
NMOD = 9
EPS = 1e-6
POOL_W = (2, 4, 8, 16)
HALO = 34
NCORES = 8
NT = 2
OWN = (SEQ // 2) // NT
TTR = OWN + HALO
TT = TTR + 1
NSUB = -(-TT // 512)
_b = TT // NSUB
_r = TT % NSUB
SUBS = []
_o = 0
for _i in range(NSUB):
    _n = _b + (1 if _i < _r else 0)
    SUBS.append((_o, _n))
    _o += _n
assert all(n == SUBS[0][1] for _, n in SUBS)
SN = SUBS[0][1]
NH = SN + 1
PAD = 16
NSLOT = 3
SLOT_ELEMS = 8192
NVEC = 21
ICW = 64


class Buf:
    __slots__ = ("w", "r")

    def __init__(self):
        self.w = None
        self.r = []


class Prog:
    ENGS = ("pe", "act", "dve", "pool", "sp")

    def __init__(self):
        self.streams = {e: [] for e in self.ENGS}
        self.cnt = {e: 0 for e in self.ENGS}
        self.pend = {e: [] for e in self.ENGS}
        self.waited = {e: {} for e in self.ENGS}
        self.dma_cnt = {}

    def _deps(self, eng, reads, writes, extra):
        deps = {}

        def add(tok):
            if tok is None:
                return
            k, v = tok
            if v > deps.get(k, 0):
                deps[k] = v
        for b in reads:
            add(b.w)
        for b in writes:
            add(b.w)
            for t in b.r:
                add(t)
        for t in extra:
            add(t)
        waits = []
        wd = self.waited[eng]
        for k, v in deps.items():
            if k == eng and eng in ("pe", "sp", "pool"):
                continue
            if wd.get(k, 0) < v:
                wd[k] = v
                waits.append((k, v))
        return waits

    def _assign(self, tok, reads, writes):
        for b in reads:
            b.r.append(tok)
        for b in writes:
            b.w = tok
            b.r = []

    def emit(self, eng, fn, reads=(), writes=(), inc=True, extra=()):
        waits = self._deps(eng, reads, writes, extra)
        tok = None
        incinfo = None
        if inc:
            self.cnt[eng] += 1
            tok = (eng, self.cnt[eng])
            incinfo = (eng, 1)
            for (r, w) in self.pend[eng]:
                self._assign(tok, r, w)
            self.pend[eng] = []
            self._assign(tok, reads, writes)
        else:
            self.pend[eng].append((tuple(reads), tuple(writes)))
        self.streams[eng].append((waits, fn, incinfo))
        return tok

    def dma(self, queue, fn, semkey, reads=(), writes=()):
        return self.dma_group(queue, [fn], semkey, reads, writes)

    def dma_group(self, queue, fns, semkey, reads=(), writes=()):
        waits = self._deps(queue, reads, writes, ())
        for fn in fns:
            self.dma_cnt[semkey] = self.dma_cnt.get(semkey, 0) + 16
            self.streams[queue].append((waits, fn, (semkey, 16)))
            waits = []
        tok = (semkey, self.dma_cnt[semkey])
        self._assign(tok, reads, writes)
        return tok

    def wait_all(self, eng, toks):
        waits = []
        wd = self.waited[eng]
        for (k, v) in toks:
            if wd.get(k, 0) < v:
                wd[k] = v
                waits.append((k, v))
        self.streams[eng].append((waits, None, None))


def build_program(n_layers=DEPTH):
    nc = bass.Bass("TRN2", target_bir_lowering=False)
    P = Prog()

    def din(name, shape, dt=F32):
        return nc.dram_tensor(name, list(shape), dt, kind="ExternalInput").ap()

    x_in = din("x_in", [NT, D, TT])
    maskd = din("maskd", [NT, 128, TT], BF16)
    invd = din("invd", [NT, 128, 4 * ICW])
    c_col = din("c_col", [128, NK])
    vecs_d = din("vecs", [128, NVEC * NK])
    bada_d = din("bada", [128, DEPTH * NMOD * NK])
    w_ada = din("w_ada", [DEPTH, D, NMOD * D])
    w_f1i = din("w_ffn1_in", [DEPTH, D, 2 * FF])
    w_f1o = din("w_ffn1_out", [DEPTH, FF, D])
    w_f2i = din("w_ffn2_in", [DEPTH, D, 2 * FF])
    w_f2o = din("w_ffn2_out", [DEPTH, FF, D])
    conv_in = din("conv_in", [2, D, 3 * D])
    conv_out = din("conv_out", [2, D, D])
    pool_w = din("pool_w", [2, 4, 512, 512])
    out_d = nc.dram_tensor("out", [NT, D, OWN], F32, kind="ExternalOutput").ap()

    es = contextlib.ExitStack()
    with es:
        def sb(name, shape, dt=F32):
            return es.enter_context(nc.sbuf_tensor(name, list(shape), dt))

        x_t = sb("x_t", [128, NK, TT])
        h_t = sb("h_t", [128, NK, NSUB, NH], BF16)
        a_flat = sb("a_t", [128, 2 * G * TT], BF16)
        a_t = a_flat[:, :].rearrange("p (a g t) -> p a g t", a=2, g=G, t=TT)
        a_f32 = a_flat.bitcast(F32)
        tmpB = [a_f32[:, q * (PAD + TT):(q + 1) * (PAD + TT)] for q in range(3)]
        rstd_t = sb("rstd_t", [128, TT])
        tmp = [sb(f"tmp{i}", [128, PAD + TT]) for i in range(3)]
        sg_t = [sb(f"sg{i}", [128, 512]) for i in range(4)]
        bias_t = sb("bias_t", [128, 4])
        mask_t = sb("mask_t", [128, TT], BF16)
        inv_t = sb("inv_t", [128, 4 * ICW])
        ta_t = sb("ta_t", [128, 2 * ICW])
        vec_t = sb("vec_t", [128, NVEC * NK])
        bada_t = sb("bada_t", [128, DEPTH * NMOD * NK])
        mods_t = sb("mods_t", [128, DEPTH * NMOD * NK])
        der_t = sb("der_t", [128, DEPTH * 6 * NK])
        ccol_t = sb("ccol_t", [128, NK])
        cact_t = sb("cact_t", [128, NK], BF16)
        ones_t = sb("ones_t", [128, 128])
        eps_t = sb("eps_t", [128, 1])
        row_t = sb("row_t", [1, 512])
        slots_t = [sb(f"slot{i}", [128, SLOT_ELEMS], BF16) for i in range(NSLOT)]
        banks_t = [es.enter_context(nc.psum_tensor(f"bank{i}", [128, 512], F32)) for i in range(8)]

        sem_names = list(Prog.ENGS) + [f"slot{i}" for i in range(NSLOT)] + ["xld", "misc", "st"]
        sems = {k: es.enter_context(nc.semaphore("s_" + k)) for k in sem_names}

        xb = [[Buf() for _ in SUBS] for _ in range(NK)]
        hb = [[Buf() for _ in SUBS] for _ in range(NK)]
        ab = [[[Buf() for _ in SUBS] for _ in range(G)] for _ in range(2)]
        rstdb = [Buf() for _ in SUBS]
        tmpb = [Buf() for _ in range(3)]
        sgb = [Buf() for _ in range(4)]
        biasb = [Buf() for _ in range(2)]
        maskb = Buf()
        invb = Buf()
        tab = [Buf() for _ in range(2)]
        tmpBb = [Buf() for _ in range(3)]
        vecb = Buf()
        badab = Buf()
        modsb = [Buf() for _ in range(DEPTH)]
        derb = [Buf() for _ in range(DEPTH)]
        rowb = [Buf() for _ in range(1)]
        ccolb = Buf()
        cactb = Buf()
        constb = Buf()
        slotb = [Buf() for _ in range(NSLOT)]
        bankb = [Buf() for _ in range(8)]
        outb = Buf()

        bank_rr = [0]

        def alloc_bank():
            i = bank_rr[0]
            bank_rr[0] = (i + 1) % 8
            return i

        def wv_rows(w2d):
            return w2d.rearrange("(k p) n -> p k n", p=128)

        plan = []

        NADA = NMOD * D // 512
        NUP = 3 * D // 512
        pts = []
        for _i in range(n_layers):
            pts += [("f", _i, 0, g, hf) for g in range(NG) for hf in range(2)]
            if _i % 2 == 0:
                pts += [("c", _i, jj) for jj in range(NK)]
            else:
                pts += [("p", _i, g) for g in range(4)]
            pts += [("f", _i, 1, g, hf) for g in range(NG) for hf in range(2)]
        rest = [(li, n) for li in range(n_layers) for n in range(NADA)][NUP:]
        assert len(pts) >= len(rest)
        ADA_AT = dict(zip(pts, rest))
        ada_done = set()

        def plan_ada_tile(li, n):
            wav = wv_rows(w_ada[li])
            plan.append(("ada", [
                (lambda s: s[:, 0:8192].rearrange("p (k c) -> p k c", k=NK, c=512),
                 wav[:, :, n * 512:(n + 1) * 512])]))

        def plan_ffn(wi, wo, t, i, which):
            wiv = wv_rows(wi)
            wov = wv_rows(wo)
            for g in range(NG):
                for half in range(2):
                    c0 = (g * G + 2 * half) * 128
                    plan.append(("ffn_in", [
                        (lambda s: s[:, 0:8192].rearrange("p (k t c) -> p k t c", k=NK, t=2, c=256)[:, :, 0, :],
                         wiv[:, :, c0:c0 + 256]),
                        (lambda s: s[:, 0:8192].rearrange("p (k t c) -> p k t c", k=NK, t=2, c=256)[:, :, 1, :],
                         wiv[:, :, FF + c0:FF + c0 + 256]),
                    ]))
                    if t == 0 and ("f", i, which, g, half) in ADA_AT:
                        plan_ada_tile(*ADA_AT[("f", i, which, g, half)])
                if g >= 1:
                    plan.append(("ffn_out", [
                        (lambda s: s[:, 0:8192].rearrange("p (f c) -> p f c", f=G, c=D),
                         wov[:, (g - 1) * G:(g - 1) * G + G, :])]))
            plan.append(("ffn_out", [
                (lambda s: s[:, 0:8192].rearrange("p (f c) -> p f c", f=G, c=D),
                 wov[:, (NG - 1) * G:NG * G, :])]))

        def plan_conv(j, t, i):
            civ = wv_rows(conv_in[j])
            cov = wv_rows(conv_out[j])
            for q in range(4):
                for jj in range(4 * q, 4 * q + 4):
                    plan.append(("conv_in", [
                        ((lambda s, tt=tt: s[:, 0:6144].rearrange("p (k t c) -> p k t c", k=NK, t=3, c=128)[:, :, tt, :]),
                         civ[:, :, tt * D + jj * 128:tt * D + jj * 128 + 128]) for tt in range(3)]))
                    if t == 0 and ("c", i, jj) in ADA_AT:
                        plan_ada_tile(*ADA_AT[("c", i, jj)])
                if q >= 1:
                    plan.append(("conv_out", [
                        (lambda s: s[:, 0:8192].rearrange("p (f c) -> p f c", f=G, c=D),
                         cov[:, (q - 1) * 4:(q - 1) * 4 + 4, :])]))
            plan.append(("conv_out", [
                (lambda s: s[:, 0:8192].rearrange("p (f c) -> p f c", f=G, c=D),
                 cov[:, 12:16, :])]))

        def plan_pool(j, t, i):
            for g in range(4):
                plan.append(("pool_w", [
                    (lambda s: s[:, 0:2048].rearrange("p (f c) -> p f c", f=4, c=512),
                     pool_w[j, g].rearrange("(k p) n -> p k n", p=128))]))
                if t == 0 and ("p", i, g) in ADA_AT:
                    plan_ada_tile(*ADA_AT[("p", i, g)])

        for n in range(NUP):
            plan_ada_tile(0, n)
        for _t in range(NT):
            for i in range(n_layers):
                plan_ffn(w_f1i[i], w_f1o[i], _t, i, 0)
                if i % 2 == 0:
                    plan_conv(i // 2, _t, i)
                else:
                    plan_pool(i // 2, _t, i)
                plan_ffn(w_f2i[i], w_f2o[i], _t, i, 1)

        ws_state = {"issued": 0, "next": 0}

        def ws_get(kind):
            i = ws_state["next"]
            assert plan[i][0] == kind, (plan[i][0], kind, i)
            ws_state["next"] = i + 1
            while ws_state["issued"] < min(len(plan), i + NSLOT):
                j = ws_state["issued"]
                sl = j % NSLOT
                fns = []
                for (dstf, src) in plan[j][1]:
                    dst = dstf(slots_t[sl])
                    fns.append(lambda e, dst=dst, src=src: e.dma_start(out=dst, in_=src))
                P.dma_group("pool", fns, f"slot{sl}", writes=[slotb[sl]])
                ws_state["issued"] = j + 1
            sl = i % NSLOT
            return slots_t[sl], slotb[sl]

        def sl_(s):
            o, n = SUBS[s]
            return slice(o, o + n)

        def vcol(v, k):
            return vec_t[:, v * NK + k:v * NK + k + 1]

        def mcol(i, m, k):
            c = (i * NMOD + m) * NK + k
            return mods_t[:, c:c + 1]

        def dcol(i, m, k):
            c = (i * 6 + m) * NK + k
            return der_t[:, c:c + 1]

        V_NF1 = lambda i: 3 * i
        V_NM = lambda i: 3 * i + 1
        V_NF2 = lambda i: 3 * i + 2
        V_FIN = 12
        V_CW = lambda j, kk: 13 + 3 * j + kk
        V_PS = lambda j: 19 + j

        P.emit("dve", lambda e: e.memset(ones_t[:, :], 1.0), writes=[constb])
        P.emit("dve", lambda e: e.memset(eps_t[:, :], EPS), writes=[constb])
        for i in range(3):
            P.emit("dve", lambda e, i=i: e.memset(tmp[i][:, 0:PAD], 0.0), writes=[tmpb[i]])
        P.dma("sp", lambda e: e.dma_start(out=vec_t[:, :], in_=vecs_d[:, :]), "misc", writes=[vecb])
        P.dma("sp", lambda e: e.dma_start(out=bada_t[:, :], in_=bada_d[:, :]), "misc", writes=[badab])
        P.dma("sp", lambda e: e.dma_start(out=ccol_t[:, :], in_=c_col[:, :]), "misc", writes=[ccolb])

        P.emit("act", lambda e: e.activation(out=cact_t[:, :], in_=ccol_t[:, :], func=AF.Silu),
               reads=[ccolb], writes=[cactb])
        ada_def = []
        ada_cnt = [0]

        def ada_flush():
            for (li, n, r) in ada_def:
                bi = alloc_bank()
                bk = banks_t[bi]
                for j in range(4):
                    P.emit("pe", (lambda e, bk=bk, j=j, r=r: e.matmul(
                        bk[:, j:j + 1], lhsT=row_t[0:1, r * 512 + j * 128:r * 512 + (j + 1) * 128],
                        rhs=ones_t[0:1, 0:1], start=True, stop=True)),
                        reads=[rowb[r], constb], writes=[bankb[bi]], inc=(j == 3))
                c0 = li * NMOD * NK + 4 * n
                P.emit("dve", (lambda e, bk=bk, c0=c0: e.tensor_tensor(
                    out=mods_t[:, c0:c0 + 4], in0=bk[:, 0:4], in1=bada_t[:, c0:c0 + 4], op=ALU.add)),
                    reads=[bankb[bi], badab], writes=[modsb[li]])
            del ada_def[:]

        def ada_rows(li, n):
            st, sbuf_ = ws_get("ada")
            v = st[:, 0:8192].rearrange("p (k c) -> p k c", k=NK, c=512)
            bi = alloc_bank()
            bk = banks_t[bi]
            for k in range(NK):
                P.emit("pe", (lambda e, bk=bk, v=v, k=k: e.matmul(
                    bk[0:1, 0:512], lhsT=cact_t[:, k:k + 1], rhs=v[:, k, :],
                    start=(k == 0), stop=(k == NK - 1))),
                    reads=[sbuf_, cactb], writes=[bankb[bi]], inc=(k == NK - 1))
            r = 0
            ada_cnt[0] += 1
            P.emit("act", (lambda e, bk=bk, r=r: e.activation(
                out=row_t[0:1, r * 512:(r + 1) * 512], in_=bk[0:1, 0:512], func=AF.Identity)),
                reads=[bankb[bi]], writes=[rowb[r]])
            ada_def.append((li, n, r))
            ada_done.add((li, n))

        def ada_derived(i, sub):
            ada_flush()
            assert all((i, n) in ada_done for n in range(NUP * (sub + 1))), (i, sub)
            vn, msc = ((V_NF1(i), 1), (V_NM(i), 4), (V_NF2(i), 7))[sub]
            c_sc = (i * NMOD + msc) * NK
            c_o = (i * 6 + sub) * NK
            P.emit("dve", (lambda e, c_sc=c_sc, c_o=c_o, vn=vn: e.scalar_tensor_tensor(
                out=der_t[:, c_o:c_o + NK], in0=mods_t[:, c_sc:c_sc + NK], scalar=1.0,
                in1=vec_t[:, vn * NK:(vn + 1) * NK], op0=ALU.add, op1=ALU.mult)),
                reads=[modsb[i], vecb], writes=[derb[i]])
            c_g = (i * NMOD + 3 * sub + 2) * NK
            c_o = (i * 6 + 3 + sub) * NK
            if sub != 1:
                P.emit("dve", (lambda e, c_g=c_g, c_o=c_o: e.tensor_scalar(
                    out=der_t[:, c_o:c_o + NK], in0=mods_t[:, c_g:c_g + NK], scalar1=0.5, scalar2=None,
                    op0=ALU.mult)), reads=[modsb[i]], writes=[derb[i]])
            elif i % 2 == 0:
                P.emit("dve", (lambda e, c_g=c_g, c_o=c_o: e.tensor_copy(
                    out=der_t[:, c_o:c_o + NK], in_=mods_t[:, c_g:c_g + NK])), reads=[modsb[i]], writes=[derb[i]])
            else:
                vp = V_PS(i // 2)
                P.emit("dve", (lambda e, c_g=c_g, c_o=c_o, vp=vp: e.tensor_tensor(
                    out=der_t[:, c_o:c_o + NK], in0=mods_t[:, c_g:c_g + NK],
                    in1=vec_t[:, vp * NK:(vp + 1) * NK], op=ALU.mult)), reads=[modsb[i], vecb], writes=[derb[i]])

        for n in range(NUP):
            ada_flush()
            ada_rows(0, n)
        ada_flush()

        def norm_stats(hook=None):
            bis = [alloc_bank() for _ in range(NSUB)]
            for k in range(NK):
                q = k % 3
                if k % 2 == 0:
                    P.emit("act", (lambda e, q=q, k=k: e.activation(
                        out=tmp[q][:, PAD:PAD + TT], in_=x_t[:, k, :], func=AF.Square)),
                        reads=list(xb[k]), writes=[tmpb[q]])
                else:
                    P.emit("dve", (lambda e, q=q, k=k: e.tensor_tensor(
                        out=tmp[q][:, PAD:PAD + TT], in0=x_t[:, k, :], in1=x_t[:, k, :], op=ALU.mult)),
                        reads=list(xb[k]), writes=[tmpb[q]])
                if hook is not None:
                    hook(k)
                for s in range(NSUB):
                    o, n = SUBS[s]
                    P.emit("pe", (lambda e, bk=banks_t[bis[s]], q=q, k=k, o=o, n=n: e.matmul(
                        bk[:, 0:n], lhsT=ones_t[:, :], rhs=tmp[q][:, PAD + o:PAD + o + n],
                        start=(k == 0), stop=(k == NK - 1))),
                        reads=[tmpb[q], constb], writes=[bankb[bis[s]]], inc=(s == NSUB - 1))
            for s in range(NSUB):
                o, n = SUBS[s]
                q = s % 2
                P.emit("act", (lambda e, bk=banks_t[bis[s]], q=q, n=n: e.activation(
                    out=sg_t[q][:, 0:n], in_=bk[:, 0:n], func=AF.Sqrt, scale=1.0 / D, bias=eps_t[:, 0:1])),
                    reads=[bankb[bis[s]], constb], writes=[sgb[q]])
                P.emit("dve", (lambda e, q=q, o=o, n=n: e.reciprocal(out=rstd_t[:, o:o + n], in_=sg_t[q][:, 0:n])),
                       reads=[sgb[q]], writes=[rstdb[s]])

        def h_prime(i, sub, k):
            xin = x_t[:, k, :].rearrange("p (s n) -> p s n", s=NSUB)
            if k % 2 == 0:
                P.emit("dve", (lambda e, k=k, xin=xin: e.tensor_scalar(
                    out=h_t[:, k, :, 0:SN], in0=xin, scalar1=dcol(i, sub, k), scalar2=None, op0=ALU.mult)),
                    reads=list(xb[k]) + [derb[i]], writes=list(hb[k]))
            else:
                P.emit("act", (lambda e, k=k, xin=xin: e.activation(
                    out=h_t[:, k, :, 0:SN], in_=xin, func=AF.Identity, scale=dcol(i, sub, k))),
                    reads=list(xb[k]) + [derb[i]], writes=list(hb[k]))

        def h_shift(i, sub):
            c_sh = (i * NMOD + (0, 3, 6)[sub]) * NK
            for s in range(NSUB):
                P.emit("dve", (lambda e, s=s: e.tensor_copy(
                    out=h_t[:, :, s, SN], in_=mods_t[:, c_sh:c_sh + NK])),
                    reads=[modsb[i]], writes=[hb[k][s] for k in range(NK)])

        def out_proj(slot_ap, slot_buf, nf, rhs_fn, rhs_bufs_fn, dcs, gate_fn, col_fn, li):
            for dc in dcs:
                for s in range(NSUB):
                    o, n = SUBS[s]
                    bi = alloc_bank()
                    bk = banks_t[bi]
                    for f in range(nf):
                        P.emit("pe", (lambda e, bk=bk, f=f, dc=dc, s=s, n=n: e.matmul(
                            bk[:, 0:n], lhsT=col_fn(slot_ap, f, dc), rhs=rhs_fn(f, s),
                            start=(f == 0), stop=(f == nf - 1))),
                            reads=[slot_buf] + rhs_bufs_fn(f, s), writes=[bankb[bi]], inc=(f == nf - 1))
                    P.emit("dve", (lambda e, bk=bk, dc=dc, o=o, n=n: e.scalar_tensor_tensor(
                        out=x_t[:, dc, o:o + n], in0=bk[:, 0:n], scalar=gate_fn(dc), in1=x_t[:, dc, o:o + n],
                        op0=ALU.mult, op1=ALU.add)),
                        reads=[bankb[bi], derb[li], xb[dc][s]], writes=[xb[dc][s]])

        def ffn(i, which, tsh):
            gsub = 3 if which == 0 else 5

            def phase_b(g):
                st, sbf = ws_get("ffn_out")
                v = st[:, 0:8192].rearrange("p (f c) -> p f c", f=G, c=D)
                gb = g % 2
                out_proj(v, sbf, G,
                         lambda f, s: a_t[:, gb, f, SUBS[s][0]:SUBS[s][0] + SN],
                         lambda f, s: [ab[gb][f][s]],
                         range(NK),
                         lambda dc: dcol(i, gsub, dc),
                         lambda vv, f, dc: vv[:, f, dc * 128:(dc + 1) * 128], i)

            unit = 0
            for g in range(NG):
                gb = g % 2
                for half in range(2):
                    st, sbf = ws_get("ffn_in")
                    v = st[:, 0:8192].rearrange("p (k t c) -> p k t c", k=NK, t=2, c=256)
                    for jj in range(2):
                        fl = 2 * half + jj
                        for s in range(NSUB):
                            o, n = SUBS[s]
                            bg = alloc_bank()
                            bu = alloc_bank()
                            for t, bi in ((0, bg), (1, bu)):
                                bk = banks_t[bi]
                                for k in range(NK):
                                    P.emit("pe", (lambda e, bk=bk, v=v, k=k, t=t, jj=jj, s=s: e.matmul(
                                        bk[:, 0:NH], lhsT=v[:, k, t, jj * 128:(jj + 1) * 128],
                                        rhs=h_t[:, k, s, 0:NH], start=(k == 0), stop=(k == NK - 1))),
                                        reads=[sbf, hb[k][s]], writes=[bankb[bi]], inc=(k == NK - 1))
                            q = unit % 2
                            pb = (unit // NSUB) % 2
                            unit += 1
                            qa, qb = 2 * q, 2 * q + 1
                            if s == 0:
                                P.emit("dve", (lambda e, bg=bg, pb=pb: e.tensor_copy(
                                    out=bias_t[:, 2 * pb:2 * pb + 1], in_=banks_t[bg][:, SN:SN + 1])),
                                    reads=[bankb[bg]], writes=[biasb[pb]])
                            P.emit("dve", (lambda e, qa=qa, bg=bg, o=o: e.tensor_tensor(
                                out=sg_t[qa][:, 0:SN], in0=banks_t[bg][:, 0:SN], in1=rstd_t[:, o:o + SN], op=ALU.mult)),
                                reads=[bankb[bg], rstdb[s]], writes=[sgb[qa]])
                            P.emit("act", (lambda e, qa=qa, pb=pb: e.activation(
                                out=sg_t[qa][:, 0:SN], in_=sg_t[qa][:, 0:SN], func=AF.Silu,
                                bias=bias_t[:, 2 * pb:2 * pb + 1])),
                                reads=[sgb[qa], biasb[pb]], writes=[sgb[qa]])
                            P.emit("dve", (lambda e, qb=qb, bu=bu, o=o: e.tensor_tensor(
                                out=sg_t[qb][:, 0:SN], in0=banks_t[bu][:, 0:SN], in1=rstd_t[:, o:o + SN], op=ALU.mult)),
                                reads=[bankb[bu], rstdb[s]], writes=[sgb[qb]])
                            P.emit("dve", (lambda e, qa=qa, qb=qb, bu=bu, gb=gb, fl=fl, o=o: e.scalar_tensor_tensor(
                                out=a_t[:, gb, fl, o:o + SN], in0=sg_t[qb][:, 0:SN], scalar=banks_t[bu][:, SN:SN + 1],
                                in1=sg_t[qa][:, 0:SN], op0=ALU.add, op1=ALU.mult)),
                                reads=[sgb[qa], sgb[qb], bankb[bu]], writes=[ab[gb][fl][s]])
                    if tsh == 0 and ("f", i, which, g, half) in ADA_AT:
                        ada_flush()
                        ada_rows(*ADA_AT[("f", i, which, g, half)])
                if g >= 1:
                    phase_b(g - 1)
            phase_b(NG - 1)
            if ada_def:
                ada_flush()

        def conv_mixer(i, tsh):
            j = i // 2
            unit = 0

            def phase_b(q4):
                st, sbf = ws_get("conv_out")
                v = st[:, 0:8192].rearrange("p (f c) -> p f c", f=G, c=D)
                gb = q4 % 2
                out_proj(v, sbf, G,
                         lambda f, s: a_t[:, gb, f, SUBS[s][0]:SUBS[s][0] + SN],
                         lambda f, s: [ab[gb][f][s]],
                         range(NK),
                         lambda dc: dcol(i, 4, dc),
                         lambda vv, f, dc: vv[:, f, dc * 128:(dc + 1) * 128], i)

            for q4 in range(4):
                gb = q4 % 2
                for jl in range(4):
                    jj = 4 * q4 + jl
                    st, sbf = ws_get("conv_in")
                    v = st[:, 0:6144].rearrange("p (k t c) -> p k t c", k=NK, t=3, c=128)
                    for s in range(NSUB):
                        o, n = SUBS[s]
                        bis = [alloc_bank() for _ in range(3)]
                        for t in range(3):
                            bk = banks_t[bis[t]]
                            for k in range(NK):
                                P.emit("pe", (lambda e, bk=bk, v=v, k=k, t=t, s=s: e.matmul(
                                    bk[:, 0:NH], lhsT=v[:, k, t, :], rhs=h_t[:, k, s, 0:NH],
                                    start=(k == 0), stop=(k == NK - 1))),
                                    reads=[sbf, hb[k][s]], writes=[bankb[bis[t]]], inc=(k == NK - 1))
                        q = unit % 2
                        pb = (unit // NSUB) % 2
                        unit += 1
                        qa, qb = 2 * q, 2 * q + 1
                        bB, bC, bV = bis
                        if s == 0:
                            P.emit("dve", (lambda e, bC=bC, pb=pb: e.tensor_copy(
                                out=bias_t[:, 2 * pb:2 * pb + 1], in_=banks_t[bC][:, SN:SN + 1])),
                                reads=[bankb[bC]], writes=[biasb[pb]])
                            P.emit("dve", (lambda e, bB=bB, pb=pb: e.tensor_copy(
                                out=bias_t[:, 2 * pb + 1:2 * pb + 2], in_=banks_t[bB][:, SN:SN + 1])),
                                reads=[bankb[bB]], writes=[biasb[pb]])
                        P.emit("dve", (lambda e, qa=qa, bC=bC, o=o: e.tensor_tensor(
                            out=sg_t[qa][:, 0:SN], in0=banks_t[bC][:, 0:SN], in1=rstd_t[:, o:o + SN], op=ALU.mult)),
                            reads=[bankb[bC], rstdb[s]], writes=[sgb[qa]])
                        P.emit("act", (lambda e, qa=qa, pb=pb: e.activation(
                            out=sg_t[qa][:, 0:SN], in_=sg_t[qa][:, 0:SN], func=AF.Identity,
                            bias=bias_t[:, 2 * pb:2 * pb + 1])),
                            reads=[sgb[qa], biasb[pb]], writes=[sgb[qa]])
                        P.emit("dve", (lambda e, qb=qb, bV=bV, o=o: e.tensor_tensor(
                            out=sg_t[qb][:, 0:SN], in0=banks_t[bV][:, 0:SN], in1=rstd_t[:, o:o + SN], op=ALU.mult)),
                            reads=[bankb[bV], rstdb[s]], writes=[sgb[qb]])
                        P.emit("dve", (lambda e, qa=qa, qb=qb, bV=bV, o=o: e.scalar_tensor_tensor(
                            out=tmp[0][:, PAD + o:PAD + o + SN], in0=sg_t[qb][:, 0:SN], scalar=banks_t[bV][:, SN:SN + 1],
                            in1=sg_t[qa][:, 0:SN], op0=ALU.add, op1=ALU.mult)),
                            reads=[sgb[qa], sgb[qb], bankb[bV]], writes=[tmpb[0]])
                        if s == 0:
                            P.emit("dve", (lambda e: e.tensor_tensor(
                                out=tmp[0][:, PAD:PAD + ICW], in0=tmp[0][:, PAD:PAD + ICW], in1=mask_t[:, 0:ICW],
                                op=ALU.mult)), reads=[tmpb[0], maskb], writes=[tmpb[0]])
                        P.emit("dve", (lambda e, bB=bB, o=o: e.tensor_tensor(
                            out=tmp[2][:, PAD + o:PAD + o + SN], in0=banks_t[bB][:, 0:SN], in1=rstd_t[:, o:o + SN],
                            op=ALU.mult)), reads=[bankb[bB], rstdb[s]], writes=[tmpb[2]])
                        P.emit("act", (lambda e, pb=pb, o=o: e.activation(
                            out=tmp[2][:, PAD + o:PAD + o + SN], in_=tmp[2][:, PAD + o:PAD + o + SN], func=AF.Identity,
                            bias=bias_t[:, 2 * pb + 1:2 * pb + 2])),
                            reads=[tmpb[2], biasb[pb]], writes=[tmpb[2]])
                    u = tmp[0]
                    y = tmp[1]
                    P.emit("dve", (lambda e, jj=jj: e.tensor_scalar(
                        out=y[:, PAD:PAD + TT], in0=u[:, PAD:PAD + TT], scalar1=vcol(V_CW(j, 2), jj), scalar2=None,
                        op0=ALU.mult)), reads=[tmpb[0], vecb], writes=[tmpb[1]])
                    for kk in (1, 2):
                        P.emit("dve", (lambda e, jj=jj, kk=kk: e.scalar_tensor_tensor(
                            out=y[:, PAD:PAD + TT], in0=u[:, PAD - kk:PAD - kk + TT], scalar=vcol(V_CW(j, 2 - kk), jj),
                            in1=y[:, PAD:PAD + TT], op0=ALU.mult, op1=ALU.add)),
                            reads=[tmpb[0], tmpb[1], vecb], writes=[tmpb[1]])
                    P.emit("dve", (lambda e, gb=gb, jl=jl: e.tensor_tensor(
                        out=a_t[:, gb, jl, :], in0=y[:, PAD:PAD + TT], in1=tmp[2][:, PAD:PAD + TT], op=ALU.mult)),
                        reads=[tmpb[1], tmpb[2]], writes=list(ab[gb][jl]))
                    if tsh == 0 and ("c", i, jj) in ADA_AT:
                        ada_flush()
                        ada_rows(*ADA_AT[("c", i, jj)])
                if q4 >= 1:
                    phase_b(q4 - 1)
            phase_b(3)

        def pool_mixer(i, tsh):
            j = i // 2
            all_ab = [b for gbl in ab for fl_ in gbl for b in fl_]
            bufsets = ((tmp[0], tmp[1], tmp[2], tmpb[0], tmpb[1], tmpb[2]),
                       (tmpB[0], tmpB[1], tmpB[2], tmpBb[0], tmpBb[1], tmpBb[2]))
            for q in range(3):
                P.emit("dve", (lambda e, q=q: e.memset(tmpB[q][:, 0:PAD], 0.0)), writes=all_ab + [tmpBb[q]])
            P.emit("dve", (lambda e: e.tensor_tensor(
                out=rstd_t[:, 0:ICW], in0=rstd_t[:, 0:ICW], in1=mask_t[:, 0:ICW], op=ALU.mult)),
                reads=[rstdb[0], maskb], writes=[rstdb[0]])

            def chain(k, c):
                g = k // 4
                w = POOL_W[g]
                hp, p0, p1, hpb, p0b, p1b = bufsets[c]
                ta = ta_t[:, c * ICW:(c + 1) * ICW]
                P.emit("dve", (lambda e: e.tensor_tensor(
                    out=hp[:, PAD:PAD + TT], in0=x_t[:, k, :], in1=rstd_t[:, :], op=ALU.mult)),
                    reads=list(xb[k]) + rstdb, writes=[hpb])
                yield
                cur, curb = hp, hpb
                sh = 1
                step = 0
                while sh < w:
                    nxt, nxtb = (p0, p0b) if step % 2 == 0 else (p1, p1b)
                    P.emit("dve", (lambda e, cur=cur, nxt=nxt, sh=sh: e.tensor_tensor(
                        out=nxt[:, PAD:PAD + TT], in0=cur[:, PAD:PAD + TT], in1=cur[:, PAD - sh:PAD - sh + TT],
                        op=ALU.add)), reads=[curb], writes=[nxtb])
                    yield
                    cur, curb = nxt, nxtb
                    sh *= 2
                    step += 1
                P.emit("dve", (lambda e, cur=cur: e.tensor_tensor(
                    out=ta, in0=cur[:, PAD:PAD + ICW], in1=inv_t[:, g * ICW:(g + 1) * ICW], op=ALU.mult)),
                    reads=[curb, invb], writes=[tab[c]])
                yield
                P.emit("dve", (lambda e: e.tensor_tensor(
                    out=ta, in0=ta, in1=hp[:, PAD:PAD + ICW], op=ALU.subtract)),
                    reads=[tab[c], hpb], writes=[tab[c]])
                yield
                P.emit("dve", (lambda e, cur=cur: e.scalar_tensor_tensor(
                    out=hp[:, PAD:PAD + TT], in0=cur[:, PAD:PAD + TT], scalar=1.0 / w, in1=hp[:, PAD:PAD + TT],
                    op0=ALU.mult, op1=ALU.subtract)),
                    reads=[curb, hpb], writes=[hpb])
                yield
                P.emit("dve", (lambda e: e.tensor_copy(out=hp[:, PAD:PAD + ICW], in_=ta)),
                       reads=[tab[c], hpb], writes=[hpb])
                yield
                P.emit("act", (lambda e: e.activation(
                    out=h_t[:, k, :, 0:SN], in_=hp[:, PAD:PAD + TT].rearrange("p (s n) -> p s n", s=NSUB),
                    func=AF.Identity, scale=dcol(i, 1, k))),
                    reads=[hpb, derb[i]], writes=list(hb[k]))
                yield

            def run_pair(k0):
                gens = [chain(k0, 0), chain(k0 + 1, 1)]
                live = [True, True]
                while any(live):
                    for c in range(2):
                        if live[c]:
                            try:
                                next(gens[c])
                            except StopIteration:
                                live[c] = False

            for g in range(4):
                run_pair(4 * g)
                run_pair(4 * g + 2)
                st, sbf = ws_get("pool_w")
                v = st[:, 0:2048].rearrange("p (f c) -> p f c", f=4, c=512)
                out_proj(v, sbf, 4,
                         lambda f, s, g=g: h_t[:, g * 4 + f, s, 0:SN],
                         lambda f, s, g=g: [hb[g * 4 + f][s]],
                         range(g * 4, g * 4 + 4),
                         lambda dc: dcol(i, 4, dc),
                         lambda vv, f, dc: vv[:, f, (dc % 4) * 128:(dc % 4 + 1) * 128], i)
                if tsh == 0 and ("p", i, g) in ADA_AT:
                    ada_flush()
                    ada_rows(*ADA_AT[("p", i, g)])
            for q in range(3):
                P.emit("dve", (lambda e, q=q: e.memset(tmpB[q][:, 0:PAD], 0.0)), writes=all_ab + [tmpBb[q]])

        for t in range(NT):
            xv = x_in[t].rearrange("(k p) n -> p k n", p=128)
            for k4 in range(0, NK, 4):
                P.dma("sp", (lambda e, k4=k4, xv=xv: e.dma_start(out=x_t[:, k4:k4 + 4, :], in_=xv[:, k4:k4 + 4, :])),
                      "xld", writes=[b for k in range(k4, k4 + 4) for b in xb[k]])
            P.dma("sp", (lambda e, t=t: e.dma_start(out=mask_t[:, :], in_=maskd[t])), "xld", writes=[maskb])
            P.dma("sp", (lambda e, t=t: e.dma_start(out=inv_t[:, :], in_=invd[t])), "xld", writes=[invb])
            for i in range(n_layers):
                if t == 0:
                    ada_derived(i, 0)
                h_shift(i, 0)
                norm_stats(lambda k, i=i: h_prime(i, 0, k))
                ffn(i, 0, t)
                if t == 0:
                    ada_derived(i, 1)
                if i % 2 == 0:
                    h_shift(i, 1)
                    norm_stats(lambda k, i=i: h_prime(i, 1, k))
                    conv_mixer(i, t)
                else:
                    norm_stats()
                    pool_mixer(i, t)
                if t == 0:
                    ada_derived(i, 2)
                h_shift(i, 2)
                norm_stats(lambda k, i=i: h_prime(i, 2, k))
                ffn(i, 1, t)
            norm_stats()
            for k in range(NK):
                q = k % 3
                P.emit("dve", (lambda e, q=q, k=k: e.tensor_tensor(
                    out=tmp[q][:, PAD:PAD + TT], in0=x_t[:, k, :], in1=rstd_t[:, :], op=ALU.mult)),
                    reads=list(xb[k]) + rstdb, writes=[tmpb[q]])
                P.emit("act", (lambda e, q=q, k=k: e.activation(
                    out=x_t[:, k, :], in_=tmp[q][:, PAD:PAD + TT], func=AF.Identity, scale=vcol(V_FIN, k))),
                    reads=[tmpb[q], vecb], writes=list(xb[k]))
            ov = out_d[t].rearrange("(k p) n -> p k n", p=128)
            for k4 in range(0, NK, 4):
                P.dma("sp", (lambda e, k4=k4, ov=ov: e.dma_start(out=ov[:, k4:k4 + 4, :], in_=x_t[:, k4:k4 + 4, HALO:HALO + OWN])),
                      "st", reads=[b for k in range(k4, k4 + 4) for b in xb[k]], writes=[Buf()])
        assert ws_state["next"] == len(plan), (ws_state, len(plan))
        final = [("st", P.dma_cnt["st"])] + [(e, P.cnt[e]) for e in ("pe", "act", "dve") if P.cnt[e] > 0]
        for k in P.dma_cnt:
            final.append((k, P.dma_cnt[k]))
        P.wait_all("sp", final)

        def run(eng, e):
            for waits, fn, incinfo in P.streams[eng]:
                for (k, v) in waits:
                    e.wait_ge(sems[k], v)
                if fn is None:
                    continue
                ins = fn(e)
                if incinfo is not None:
                    ins.then_inc(sems[incinfo[0]], incinfo[1])

        with nc.Block() as block:
            @block.tensor
            def _(e):
                run("pe", e)

            @block.scalar
            def _(e):
                run("act", e)

            @block.vector
            def _(e):
                run("dve", e)

            @block.gpsimd
            def _(e):
                run("pool", e)

            @block.sync
            def _(e):
                run("sp", e)
    return nc, {e: len(P.streams[e]) for e in P.ENGS}


def _cols(v):
    return np.ascontiguousarray(np.asarray(v, np.float32).reshape(NK, 128).T)


def _prep_inputs(x, c, norm_ffn1, norm_mix, norm_ffn2, w_ada, b_ada, w_ffn1_in, w_ffn1_out,
                 w_ffn2_in, w_ffn2_out, conv_in, conv_w, conv_out, pool_w, pool_scale, final_norm):
    x = np.asarray(x, np.float32)
    vec_list = []
    for i in range(DEPTH):
        vec_list += [_cols(norm_ffn1[i]), _cols(norm_mix[i]), _cols(norm_ffn2[i])]
    vec_list.append(_cols(final_norm))
    for j in range(2):
        for kk in range(3):
            vec_list.append(_cols(np.asarray(conv_w)[j, kk]))
    for j in range(2):
        vec_list.append(_cols(np.asarray(pool_scale)[j]))
    vecs = np.ascontiguousarray(np.concatenate(vec_list, axis=1))
    b_ada = np.asarray(b_ada, np.float32)
    bada = np.ascontiguousarray(np.concatenate(
        [b_ada[i].reshape(NMOD * NK, 128).T for i in range(DEPTH)], axis=1))
    shared = {
        "vecs": vecs, "bada": bada,
        "w_ada": np.asarray(w_ada, np.float32),
        "w_ffn1_in": np.asarray(w_ffn1_in, np.float32), "w_ffn1_out": np.asarray(w_ffn1_out, np.float32),
        "w_ffn2_in": np.asarray(w_ffn2_in, np.float32), "w_ffn2_out": np.asarray(w_ffn2_out, np.float32),
        "conv_in": np.asarray(conv_in, np.float32), "conv_out": np.asarray(conv_out, np.float32),
        "pool_w": np.asarray(pool_w, np.float32),
    }
    in_maps = []
    for core in range(NCORES):
        b = core // 2
        half = core % 2
        xin = np.zeros((NT, D, TT), np.float32)
        mask = np.zeros((NT, 128, TT), ml_dtypes.bfloat16)
        inv = np.ones((NT, 128, 4, ICW), np.float32)
        for t in range(NT):
            start = (half * NT + t) * OWN - HALO
            lo = max(start, 0)
            xin[t][:, lo - start:TTR] = x[b, lo:start + TTR, :].T
            mask[t][:, lo - start:] = 1.0
            pos = start + np.arange(ICW)
            for g, w in enumerate(POOL_W):
                cnt = np.minimum(np.maximum(pos, 0) + 1, w).astype(np.float32)
                inv[t][:, g, :] = (1.0 / cnt)[None, :]
        m = dict(shared)
        m["x_in"] = xin
        m["maskd"] = mask
        m["invd"] = np.ascontiguousarray(inv.reshape(NT, 128, 4 * ICW))
        m["c_col"] = _cols(np.asarray(c, np.float32)[b])
        in_maps.append(m)
    return in_maps


_CACHE = {}


def kernel(**inputs):
    in_maps = _prep_inputs(**inputs)
    if "nc" not in _CACHE:
        _CACHE["nc"] = build_program(DEPTH)[0]
    nc = _CACHE["nc"]
    res = run_bass_kernel_spmd(nc, in_maps, core_ids=list(range(NCORES)))
    out = np.empty((BATCH, SEQ, D), np.float32)
    for core in range(NCORES):
        b = core // 2
        half = core % 2
        o = res.results[core]["out"]
        for t in range(NT):
            s0 = (half * NT + t) * OWN
            out[b, s0:s0 + OWN, :] = o[t].T
    return out
```

```python
import contextlib
import numpy as np
import ml_dtypes
import concourse.bass as bass
import concourse.mybir as mybir
from concourse.bass_utils import run_bass_kernel_spmd

F32 = mybir.dt.float32
BF16 = mybir.dt.bfloat16
AF = mybir.ActivationFunctionType
ALU = mybir.AluOpType

D = 2048
BATCH = 4
SEQ = 4096
DEPTH = 4
FF = 5632
NK = D // 128
NFC = FF // 128
G = 4
NG = NFC // G
NMOD = 9
EPS = 1e-6
POOL_W = (2, 4, 8, 16)
HALO = 34
NCORES = 8
NT = 2
OWN = (SEQ // 2) // NT
TTR = OWN + HALO
TT = TTR + 1
NSUB = -(-TT // 512)
_b = TT // NSUB
_r = TT % NSUB
SUBS = []
_o = 0
for _i in range(NSUB):
    _n = _b + (1 if _i < _r else 0)
    SUBS.append((_o, _n))
    _o += _n
assert all(n == SUBS[0][1] for _, n in SUBS)
SN = SUBS[0][1]
NH = SN + 1
PAD = 16
NSLOT = 3
SLOT_ELEMS = 8192
NVEC = 21
ICW = 64


class Buf:
    __slots__ = ("w", "r")

    def __init__(self):
        self.w = None
        self.r = []


class Prog:
    ENGS = ("pe", "act", "dve", "pool", "sp")

    def __init__(self):
        self.streams = {e: [] for e in self.ENGS}
        self.cnt = {e: 0 for e in self.ENGS}
        self.pend = {e: [] for e in self.ENGS}
        self.waited = {e: {} for e in self.ENGS}
        self.dma_cnt = {}

    def _deps(self, eng, reads, writes, extra):
        deps = {}

        def add(tok):
            if tok is None:
                return
            k, v = tok
            if v > deps.get(k, 0):
                deps[k] = v
        for b in reads:
            add(b.w)
        for b in writes:
            add(b.w)
            for t in b.r:
                add(t)
        for t in extra:
            add(t)
        waits = []
        wd = self.waited[eng]
        for k, v in deps.items():
            if k == eng and eng in ("pe", "sp", "pool"):
                continue
            if wd.get(k, 0) < v:
                wd[k] = v
                waits.append((k, v))
        return waits

    def _assign(self, tok, reads, writes):
        for b in reads:
            b.r.append(tok)
        for b in writes:
            b.w = tok
            b.r = []

    def emit(self, eng, fn, reads=(), writes=(), inc=True, extra=()):
        waits = self._deps(eng, reads, writes, extra)
        tok = None
        incinfo = None
        if inc:
            self.cnt[eng] += 1
            tok = (eng, self.cnt[eng])
            incinfo = (eng, 1)
            for (r, w) in self.pend[eng]:
                self._assign(tok, r, w)
            self.pend[eng] = []
            self._assign(tok, reads, writes)
        else:
            self.pend[eng].append((tuple(reads), tuple(writes)))
        self.streams[eng].append((waits, fn, incinfo))
        return tok

    def dma(self, queue, fn, semkey, reads=(), writes=()):
        return self.dma_group(queue, [fn], semkey, reads, writes)

    def dma_group(self, queue, fns, semkey, reads=(), writes=()):
        waits = self._deps(queue, reads, writes, ())
        for fn in fns:
            self.dma_cnt[semkey] = self.dma_cnt.get(semkey, 0) + 16
            self.streams[queue].append((waits, fn, (semkey, 16)))
            waits = []
        tok = (semkey, self.dma_cnt[semkey])
        self._assign(tok, reads, writes)
        return tok

    def wait_all(self, eng, toks):
        waits = []
        wd = self.waited[eng]
        for (k, v) in toks:
            if wd.get(k, 0) < v:
                wd[k] = v
                waits.append((k, v))
        self.streams[eng].append((waits, None, None))


def build_program(n_layers=DEPTH):
    nc = bass.Bass("TRN2", target_bir_lowering=False)
    P = Prog()

    def din(name, shape, dt=F32):
        return nc.dram_tensor(name, list(shape), dt, kind="ExternalInput").ap()

    x_in = din("x_in", [NT, D, TT])
    maskd = din("maskd", [NT, 128, TT], BF16)
    invd = din("invd", [NT, 128, 4 * ICW])
    c_col = din("c_col", [128, NK])
    vecs_d = din("vecs", [128, NVEC * NK])
    bada_d = din("bada", [128, DEPTH * NMOD * NK])
    w_ada = din("w_ada", [DEPTH, D, NMOD * D])
    w_f1i = din("w_ffn1_in", [DEPTH, D, 2 * FF])
    w_f1o = din("w_ffn1_out", [DEPTH, FF, D])
    w_f2i = din("w_ffn2_in", [DEPTH, D, 2 * FF])
    w_f2o = din("w_ffn2_out", [DEPTH, FF, D])
    conv_in = din("conv_in", [2, D, 3 * D])
    conv_out = din("conv_out", [2, D, D])
    pool_w = din("pool_w", [2, 4, 512, 512])
    out_d = nc.dram_tensor("out", [NT, D, OWN], F32, kind="ExternalOutput").ap()

    es = contextlib.ExitStack()
    with es:
        def sb(name, shape, dt=F32):
            return es.enter_context(nc.sbuf_tensor(name, list(shape), dt))

        x_t = sb("x_t", [128, NK, TT])
        h_t = sb("h_t", [128, NK, NSUB, NH], BF16)
        a_flat = sb("a_t", [128, 2 * G * TT], BF16)
        a_t = a_flat[:, :].rearrange("p (a g t) -> p a g t", a=2, g=G, t=TT)
        a_f32 = a_flat.bitcast(F32)
        tmpB = [a_f32[:, q * (PAD + TT):(q + 1) * (PAD + TT)] for q in range(3)]
        rstd_t = sb("rstd_t", [128, TT])
        tmp = [sb(f"tmp{i}", [128, PAD + TT]) for i in range(3)]
        sg_t = [sb(f"sg{i}", [128, 512]) for i in range(4)]
        bias_t = sb("bias_t", [128, 4])
        mask_t = sb("mask_t", [128, TT], BF16)
        inv_t = sb("inv_t", [128, 4 * ICW])
        ta_t = sb("ta_t", [128, 2 * ICW])
        vec_t = sb("vec_t", [128, NVEC * NK])
        bada_t = sb("bada_t", [128, DEPTH * NMOD * NK])
        mods_t = sb("mods_t", [128, DEPTH * NMOD * NK])
        der_t = sb("der_t", [128, DEPTH * 6 * NK])
        ccol_t = sb("ccol_t", [128, NK])
        cact_t = sb("cact_t", [128, NK], BF16)
        ones_t = sb("ones_t", [128, 128])
        eps_t = sb("eps_t", [128, 1])
        row_t = sb("row_t", [1, 512])
        slots_t = [sb(f"slot{i}", [128, SLOT_ELEMS], BF16) for i in range(NSLOT)]
        banks_t = [es.enter_context(nc.psum_tensor(f"bank{i}", [128, 512], F32)) for i in range(8)]

        sem_names = list(Prog.ENGS) + [f"slot{i}" for i in range(NSLOT)] + ["xld", "misc", "st"]
        sems = {k: es.enter_context(nc.semaphore("s_" + k)) for k in sem_names}

        xb = [[Buf() for _ in SUBS] for _ in range(NK)]
        hb = [[Buf() for _ in SUBS] for _ in range(NK)]
        ab = [[[Buf() for _ in SUBS] for _ in range(G)] for _ in range(2)]
        rstdb = [Buf() for _ in SUBS]
        tmpb = [Buf() for _ in range(3)]
        sgb = [Buf() for _ in range(4)]
        biasb = [Buf() for _ in range(2)]
        maskb = Buf()
        invb = Buf()
        tab = [Buf() for _ in range(2)]
        tmpBb = [Buf() for _ in range(3)]
        vecb = Buf()
        badab = Buf()
        modsb = [Buf() for _ in range(DEPTH)]
        derb = [Buf() for _ in range(DEPTH)]
        rowb = [Buf() for _ in range(1)]
        ccolb = Buf()
        cactb = Buf()
        constb = Buf()
        slotb = [Buf() for _ in range(NSLOT)]
        bankb = [Buf() for _ in range(8)]
        outb = Buf()

        bank_rr = [0]

        def alloc_bank():
            i = bank_rr[0]
            bank_rr[0] = (i + 1) % 8
            return i

        def wv_rows(w2d):
            return w2d.rearrange("(k p) n -> p k n", p=128)

        plan = []

        NADA = NMOD * D // 512
        NUP = 3 * D // 512
        pts = []
        for _i in range(n_layers):
            pts += [("f", _i, 0, g, hf) for g in range(NG) for hf in range(2)]
            if _i % 2 == 0:
                pts += [("c", _i, jj) for jj in range(NK)]
            else:
                pts += [("p", _i, g) for g in range(4)]
            pts += [("f", _i, 1, g, hf) for g in range(NG) for hf in range(2)]
        rest = [(li, n) for li in range(n_layers) for n in range(NADA)][NUP:]
        assert len(pts) >= len(rest)
        ADA_AT = dict(zip(pts, rest))
        ada_done = set()

        def plan_ada_tile(li, n):
            wav = wv_rows(w_ada[li])
            plan.append(("ada", [
                (lambda s: s[:, 0:8192].rearrange("p (k c) -> p k c", k=NK, c=512),
                 wav[:, :, n * 512:(n + 1) * 512])]))

        def plan_ffn(wi, wo, t, i, which):
            wiv = wv_rows(wi)
            wov = wv_rows(wo)
            for g in range(NG):
                for half in range(2):
                    c0 = (g * G + 2 * half) * 128
                    plan.append(("ffn_in", [
                        (lambda s: s[:, 0:8192].rearrange("p (k t c) -> p k t c", k=NK, t=2, c=256)[:, :, 0, :],
                         wiv[:, :, c0:c0 + 256]),
                        (lambda s: s[:, 0:8192].rearrange("p (k t c) -> p k t c", k=NK, t=2, c=256)[:, :, 1, :],
                         wiv[:, :, FF + c0:FF + c0 + 256]),
                    ]))
                    if t == 0 and ("f", i, which, g, half) in ADA_AT:
                        plan_ada_tile(*ADA_AT[("f", i, which, g, half)])
                if g >= 1:
                    plan.append(("ffn_out", [
                        (lambda s: s[:, 0:8192].rearrange("p (f c) -> p f c", f=G, c=D),
                         wov[:, (g - 1) * G:(g - 1) * G + G, :])]))
            plan.append(("ffn_out", [
                (lambda s: s[:, 0:8192].rearrange("p (f c) -> p f c", f=G, c=D),
                 wov[:, (NG - 1) * G:NG * G, :])]))

        def plan_conv(j, t, i):
            civ = wv_rows(conv_in[j])
            cov = wv_rows(conv_out[j])
            for q in range(4):
                for jj in range(4 * q, 4 * q + 4):
                    plan.append(("conv_in", [
                        ((lambda s, tt=tt: s[:, 0:6144].rearrange("p (k t c) -> p k t c", k=NK, t=3, c=128)[:, :, tt, :]),
                         civ[:, :, tt * D + jj * 128:tt * D + jj * 128 + 128]) for tt in range(3)]))
                    if t == 0 and ("c", i, jj) in ADA_AT:
                        plan_ada_tile(*ADA_AT[("c", i, jj)])
                if q >= 1:
                    plan.append(("conv_out", [
                        (lambda s: s[:, 0:8192].rearrange("p (f c) -> p f c", f=G, c=D),
                         cov[:, (q - 1) * 4:(q - 1) * 4 + 4, :])]))
            plan.append(("conv_out", [
                (lambda s: s[:, 0:8192].rearrange("p (f c) -> p f c", f=G, c=D),
                 cov[:, 12:16, :])]))

        def plan_pool(j, t, i):
            for g in range(4):
                plan.append(("pool_w", [
                    (lambda s: s[:, 0:2048].rearrange("p (f c) -> p f c", f=4, c=512),
                     pool_w[j, g].rearrange("(k p) n -> p k n", p=128))]))
                if t == 0 and ("p", i, g) in ADA_AT:
                    plan_ada_tile(*ADA_AT[("p", i, g)])

        for n in range(NUP):
            plan_ada_tile(0, n)
        for _t in range(NT):
            for i in range(n_layers):
                plan_ffn(w_f1i[i], w_f1o[i], _t, i, 0)
                if i % 2 == 0:
                    plan_conv(i // 2, _t, i)
                else:
                    plan_pool(i // 2, _t, i)
                plan_ffn(w_f2i[i], w_f2o[i], _t, i, 1)

        ws_state = {"issued": 0, "next": 0}

        def ws_get(kind):
            i = ws_state["next"]
            assert plan[i][0] == kind, (plan[i][0], kind, i)
            ws_state["next"] = i + 1
            while ws_state["issued"] < min(len(plan), i + NSLOT):
                j = ws_state["issued"]
                sl = j % NSLOT
                fns = []
                for (dstf, src) in plan[j][1]:
                    dst = dstf(slots_t[sl])
                    fns.append(lambda e, dst=dst, src=src: e.dma_start(out=dst, in_=src))
                P.dma_group("pool", fns, f"slot{sl}", writes=[slotb[sl]])
                ws_state["issued"] = j + 1
            sl = i % NSLOT
            return slots_t[sl], slotb[sl]

        def sl_(s):
            o, n = SUBS[s]
            return slice(o, o + n)

        def vcol(v, k):
            return vec_t[:, v * NK + k:v * NK + k + 1]

        def mcol(i, m, k):
            c = (i * NMOD + m) * NK + k
            return mods_t[:, c:c + 1]

        def dcol(i, m, k):
            c = (i * 6 + m) * NK + k
            return der_t[:, c:c + 1]

        V_NF1 = lambda i: 3 * i
        V_NM = lambda i: 3 * i + 1
        V_NF2 = lambda i: 3 * i + 2
        V_FIN = 12
        V_CW = lambda j, kk: 13 + 3 * j + kk
        V_PS = lambda j: 19 + j

        P.emit("dve", lambda e: e.memset(ones_t[:, :], 1.0), writes=[constb])
        P.emit("dve", lambda e: e.memset(eps_t[:, :], EPS), writes=[constb])
        for i in range(3):
            P.emit("dve", lambda e, i=i: e.memset(tmp[i][:, 0:PAD], 0.0), writes=[tmpb[i]])
        P.dma("sp", lambda e: e.dma_start(out=vec_t[:, :], in_=vecs_d[:, :]), "misc", writes=[vecb])
        P.dma("sp", lambda e: e.dma_start(out=bada_t[:, :], in_=bada_d[:, :]), "misc", writes=[badab])
        P.dma("sp", lambda e: e.dma_start(out=ccol_t[:, :], in_=c_col[:, :]), "misc", writes=[ccolb])

        P.emit("act", lambda e: e.activation(out=cact_t[:, :], in_=ccol_t[:, :], func=AF.Silu),
               reads=[ccolb], writes=[cactb])
        ada_def = []
        ada_cnt = [0]

        def ada_flush():
            for (li, n, r) in ada_def:
                bi = alloc_bank()
                bk = banks_t[bi]
                for j in range(4):
                    P.emit("pe", (lambda e, bk=bk, j=j, r=r: e.matmul(
                        bk[:, j:j + 1], lhsT=row_t[0:1, r * 512 + j * 128:r * 512 + (j + 1) * 128],
                        rhs=ones_t[0:1, 0:1], start=True, stop=True)),
                        reads=[rowb[r], constb], writes=[bankb[bi]], inc=(j == 3))
                c0 = li * NMOD * NK + 4 * n
                P.emit("dve", (lambda e, bk=bk, c0=c0: e.tensor_tensor(
                    out=mods_t[:, c0:c0 + 4], in0=bk[:, 0:4], in1=bada_t[:, c0:c0 + 4], op=ALU.add)),
                    reads=[bankb[bi], badab], writes=[modsb[li]])
            del ada_def[:]

        def ada_rows(li, n):
            st, sbuf_ = ws_get("ada")
            v = st[:, 0:8192].rearrange("p (k c) -> p k c", k=NK, c=512)
            bi = alloc_bank()
            bk = banks_t[bi]
            for k in range(NK):
                P.emit("pe", (lambda e, bk=bk, v=v, k=k: e.matmul(
                    bk[0:1, 0:512], lhsT=cact_t[:, k:k + 1], rhs=v[:, k, :],
                    start=(k == 0), stop=(k == NK - 1))),
                    reads=[sbuf_, cactb], writes=[bankb[bi]], inc=(k == NK - 1))
            r = 0
            ada_cnt[0] += 1
            P.emit("act", (lambda e, bk=bk, r=r: e.activation(
                out=row_t[0:1, r * 512:(r + 1) * 512], in_=bk[0:1, 0:512], func=AF.Identity)),
                reads=[bankb[bi]], writes=[rowb[r]])
            ada_def.append((li, n, r))
            ada_done.add((li, n))

        def ada_derived(i, sub):
            ada_flush()
            assert all((i, n) in ada_done for n in range(NUP * (sub + 1))), (i, sub)
            vn, msc = ((V_NF1(i), 1), (V_NM(i), 4), (V_NF2(i), 7))[sub]
            c_sc = (i * NMOD + msc) * NK
            c_o = (i * 6 + sub) * NK
            P.emit("dve", (lambda e, c_sc=c_sc, c_o=c_o, vn=vn: e.scalar_tensor_tensor(
                out=der_t[:, c_o:c_o + NK], in0=mods_t[:, c_sc:c_sc + NK], scalar=1.0,
                in1=vec_t[:, vn * NK:(vn + 1) * NK], op0=ALU.add, op1=ALU.mult)),
                reads=[modsb[i], vecb], writes=[derb[i]])
            c_g = (i * NMOD + 3 * sub + 2) * NK
            c_o = (i * 6 + 3 + sub) * NK
            if sub != 1:
                P.emit("dve", (lambda e, c_g=c_g, c_o=c_o: e.tensor_scalar(
                    out=der_t[:, c_o:c_o + NK], in0=mods_t[:, c_g:c_g + NK], scalar1=0.5, scalar2=None,
                    op0=ALU.mult)), reads=[modsb[i]], writes=[derb[i]])
            elif i % 2 == 0:
                P.emit("dve", (lambda e, c_g=c_g, c_o=c_o: e.tensor_copy(
                    out=der_t[:, c_o:c_o + NK], in_=mods_t[:, c_g:c_g + NK])), reads=[modsb[i]], writes=[derb[i]])
            else:
                vp = V_PS(i // 2)
                P.emit("dve", (lambda e, c_g=c_g, c_o=c_o, vp=vp: e.tensor_tensor(
                    out=der_t[:, c_o:c_o + NK], in0=mods_t[:, c_g:c_g + NK],
                    in1=vec_t[:, vp * NK:(vp + 1) * NK], op=ALU.mult)), reads=[modsb[i], vecb], writes=[derb[i]])

        for n in range(NUP):
            ada_flush()
            ada_rows(0, n)
        ada_flush()

        def norm_stats(hook=None):
            bis = [alloc_bank() for _ in range(NSUB)]
            for k in range(NK):
                q = k % 3
                if k % 2 == 0:
                    P.emit("act", (lambda e, q=q, k=k: e.activation(
                        out=tmp[q][:, PAD:PAD + TT], in_=x_t[:, k, :], func=AF.Square)),
                        reads=list(xb[k]), writes=[tmpb[q]])
                else:
                    P.emit("dve", (lambda e, q=q, k=k: e.tensor_tensor(
                        out=tmp[q][:, PAD:PAD + TT], in0=x_t[:, k, :], in1=x_t[:, k, :], op=ALU.mult)),
                        reads=list(xb[k]), writes=[tmpb[q]])
                if hook is not None:
                    hook(k)
                for s in range(NSUB):
                    o, n = SUBS[s]
                    P.emit("pe", (lambda e, bk=banks_t[bis[s]], q=q, k=k, o=o, n=n: e.matmul(
                        bk[:, 0:n], lhsT=ones_t[:, :], rhs=tmp[q][:, PAD + o:PAD + o + n],
                        start=(k == 0), stop=(k == NK - 1))),
                        reads=[tmpb[q], constb], writes=[bankb[bis[s]]], inc=(s == NSUB - 1))
            for s in range(NSUB):
                o, n = SUBS[s]
                q = s
                P.emit("act", (lambda e, bk=banks_t[bis[s]], q=q, n=n: e.activation(
                    out=sg_t[q][:, 0:n], in_=bk[:, 0:n], func=AF.Sqrt, scale=1.0 / D, bias=eps_t[:, 0:1])),
                    reads=[bankb[bis[s]], constb], writes=[sgb[q]])
                P.emit("dve", (lambda e, q=q, o=o, n=n: e.reciprocal(out=rstd_t[:, o:o + n], in_=sg_t[q][:, 0:n])),
                       reads=[sgb[q]], writes=[rstdb[s]])

        def h_prime(i, sub, k):
            xin = x_t[:, k, :].rearrange("p (s n) -> p s n", s=NSUB)
            if k % 2 == 0:
                P.emit("dve", (lambda e, k=k, xin=xin: e.tensor_scalar(
                    out=h_t[:, k, :, 0:SN], in0=xin, scalar1=dcol(i, sub, k), scalar2=None, op0=ALU.mult)),
                    reads=list(xb[k]) + [derb[i]], writes=list(hb[k]))
            else:
                P.emit("act", (lambda e, k=k, xin=xin: e.activation(
                    out=h_t[:, k, :, 0:SN], in_=xin, func=AF.Identity, scale=dcol(i, sub, k))),
                    reads=list(xb[k]) + [derb[i]], writes=list(hb[k]))

        def h_shift(i, sub):
            c_sh = (i * NMOD + (0, 3, 6)[sub]) * NK
            for s in range(NSUB):
                P.emit("dve", (lambda e, s=s: e.tensor_copy(
                    out=h_t[:, :, s, SN], in_=mods_t[:, c_sh:c_sh + NK])),
                    reads=[modsb[i]], writes=[hb[k][s] for k in range(NK)])

        def out_proj(slot_ap, slot_buf, nf, rhs_fn, rhs_bufs_fn, dcs, gate_fn, col_fn, li):
            for dc in dcs:
                for s in range(NSUB):
                    o, n = SUBS[s]
                    bi = alloc_bank()
                    bk = banks_t[bi]
                    for f in range(nf):
                        P.emit("pe", (lambda e, bk=bk, f=f, dc=dc, s=s, n=n: e.matmul(
                            bk[:, 0:n], lhsT=col_fn(slot_ap, f, dc), rhs=rhs_fn(f, s),
                            start=(f == 0), stop=(f == nf - 1))),
                            reads=[slot_buf] + rhs_bufs_fn(f, s), writes=[bankb[bi]], inc=(f == nf - 1))
                    P.emit("dve", (lambda e, bk=bk, dc=dc, o=o, n=n: e.scalar_tensor_tensor(
                        out=x_t[:, dc, o:o + n], in0=bk[:, 0:n], scalar=gate_fn(dc), in1=x_t[:, dc, o:o + n],
                        op0=ALU.mult, op1=ALU.add)),
                        reads=[bankb[bi], derb[li], xb[dc][s]], writes=[xb[dc][s]])

        def ffn(i, which, tsh):
            gsub = 3 if which == 0 else 5

            def phase_b(g):
                st, sbf = ws_get("ffn_out")
                v = st[:, 0:8192].rearrange("p (f c) -> p f c", f=G, c=D)
                gb = g % 2
                out_proj(v, sbf, G,
                         lambda f, s: a_t[:, gb, f, SUBS[s][0]:SUBS[s][0] + SN],
                         lambda f, s: [ab[gb][f][s]],
                         range(NK),
                         lambda dc: dcol(i, gsub, dc),
                         lambda vv, f, dc: vv[:, f, dc * 128:(dc + 1) * 128], i)

            unit = 0
            for g in range(NG):
                gb = g % 2
                for half in range(2):
                    st, sbf = ws_get("ffn_in")
                    v = st[:, 0:8192].rearrange("p (k t c) -> p k t c", k=NK, t=2, c=256)
                    for jj in range(2):
                        fl = 2 * half + jj
                        for s in range(NSUB):
                            o, n = SUBS[s]
                            bg = alloc_bank()
                            bu = alloc_bank()
                            for t, bi in ((0, bg), (1, bu)):
                                bk = banks_t[bi]
                                for k in range(NK):
                                    P.emit("pe", (lambda e, bk=bk, v=v, k=k, t=t, jj=jj, s=s: e.matmul(
                                        bk[:, 0:NH], lhsT=v[:, k, t, jj * 128:(jj + 1) * 128],
                                        rhs=h_t[:, k, s, 0:NH], start=(k == 0), stop=(k == NK - 1))),
                                        reads=[sbf, hb[k][s]], writes=[bankb[bi]], inc=(k == NK - 1))
                            q = unit % 2
                            pb = (unit // NSUB) % 2
                            unit += 1
                            qa, qb = 2 * q, 2 * q + 1
                            if s == 0:
                                P.emit("dve", (lambda e, bg=bg, pb=pb: e.tensor_copy(
                                    out=bias_t[:, 2 * pb:2 * pb + 1], in_=banks_t[bg][:, SN:SN + 1])),
                                    reads=[bankb[bg]], writes=[biasb[pb]])
                            P.emit("dve", (lambda e, qa=qa, bg=bg, o=o: e.tensor_tensor(
                                out=sg_t[qa][:, 0:SN], in0=banks_t[bg][:, 0:SN], in1=rstd_t[:, o:o + SN], op=ALU.mult)),
                                reads=[bankb[bg], rstdb[s]], writes=[sgb[qa]])
                            P.emit("act", (lambda e, qa=qa, pb=pb: e.activation(
                                out=sg_t[qa][:, 0:SN], in_=sg_t[qa][:, 0:SN], func=AF.Silu,
                                bias=bias_t[:, 2 * pb:2 * pb + 1])),
                                reads=[sgb[qa], biasb[pb]], writes=[sgb[qa]])
                            P.emit("dve", (lambda e, qb=qb, bu=bu, o=o: e.tensor_tensor(
                                out=sg_t[qb][:, 0:SN], in0=banks_t[bu][:, 0:SN], in1=rstd_t[:, o:o + SN], op=ALU.mult)),
                                reads=[bankb[bu], rstdb[s]], writes=[sgb[qb]])
                            P.emit("dve", (lambda e, qa=qa, qb=qb, bu=bu, gb=gb, fl=fl, o=o: e.scalar_tensor_tensor(
                                out=a_t[:, gb, fl, o:o + SN], in0=sg_t[qb][:, 0:SN], scalar=banks_t[bu][:, SN:SN + 1],
                                in1=sg_t[qa][:, 0:SN], op0=ALU.add, op1=ALU.mult)),
                                reads=[sgb[qa], sgb[qb], bankb[bu]], writes=[ab[gb][fl][s]])
                    if tsh == 0 and ("f", i, which, g, half) in ADA_AT:
                        ada_flush()
                        ada_rows(*ADA_AT[("f", i, which, g, half)])
                if g >= 1:
                    phase_b(g - 1)
            phase_b(NG - 1)
            if ada_def:
                ada_flush()

        def conv_mixer(i, tsh):
            j = i // 2
            unit = 0

            def phase_b(q4):
                st, sbf = ws_get("conv_out")
                v = st[:, 0:8192].rearrange("p (f c) -> p f c", f=G, c=D)
                gb = q4 % 2
                out_proj(v, sbf, G,
                         lambda f, s: a_t[:, gb, f, SUBS[s][0]:SUBS[s][0] + SN],
                         lambda f, s: [ab[gb][f][s]],
                         range(NK),
                         lambda dc: dcol(i, 4, dc),
                         lambda vv, f, dc: vv[:, f, dc * 128:(dc + 1) * 128], i)

            for q4 in range(4):
                gb = q4 % 2
                for jl in range(4):
                    jj = 4 * q4 + jl
                    st, sbf = ws_get("conv_in")
                    v = st[:, 0:6144].rearrange("p (k t c) -> p k t c", k=NK, t=3, c=128)
                    for s in range(NSUB):
                        o, n = SUBS[s]
                        bis = [alloc_bank() for _ in range(3)]
                        for t in range(3):
                            bk = banks_t[bis[t]]
                            for k in range(NK):
                                P.emit("pe", (lambda e, bk=bk, v=v, k=k, t=t, s=s: e.matmul(
                                    bk[:, 0:NH], lhsT=v[:, k, t, :], rhs=h_t[:, k, s, 0:NH],
                                    start=(k == 0), stop=(k == NK - 1))),
                                    reads=[sbf, hb[k][s]], writes=[bankb[bis[t]]], inc=(k == NK - 1))
                        q = unit % 2
                        pb = (unit // NSUB) % 2
                        unit += 1
                        qa, qb = 2 * q, 2 * q + 1
                        bB, bC, bV = bis
                        if s == 0:
                            P.emit("dve", (lambda e, bC=bC, pb=pb: e.tensor_copy(
                                out=bias_t[:, 2 * pb:2 * pb + 1], in_=banks_t[bC][:, SN:SN + 1])),
                                reads=[bankb[bC]], writes=[biasb[pb]])
                            P.emit("dve", (lambda e, bB=bB, pb=pb: e.tensor_copy(
                                out=bias_t[:, 2 * pb + 1:2 * pb + 2], in_=banks_t[bB][:, SN:SN + 1])),
                                reads=[bankb[bB]], writes=[biasb[pb]])
                        P.emit("dve", (lambda e, qa=qa, bC=bC, o=o: e.tensor_tensor(
                            out=sg_t[qa][:, 0:SN], in0=banks_t[bC][:, 0:SN], in1=rstd_t[:, o:o + SN], op=ALU.mult)),
                            reads=[bankb[bC], rstdb[s]], writes=[sgb[qa]])
                        P.emit("act", (lambda e, qa=qa, pb=pb: e.activation(
                            out=sg_t[qa][:, 0:SN], in_=sg_t[qa][:, 0:SN], func=AF.Identity,
                            bias=bias_t[:, 2 * pb:2 * pb + 1])),
                            reads=[sgb[qa], biasb[pb]], writes=[sgb[qa]])
                        P.emit("dve", (lambda e, qb=qb, bV=bV, o=o: e.tensor_tensor(
                            out=sg_t[qb][:, 0:SN], in0=banks_t[bV][:, 0:SN], in1=rstd_t[:, o:o + SN], op=ALU.mult)),
                            reads=[bankb[bV], rstdb[s]], writes=[sgb[qb]])
                        P.emit("dve", (lambda e, qa=qa, qb=qb, bV=bV, o=o: e.scalar_tensor_tensor(
                            out=tmp[0][:, PAD + o:PAD + o + SN], in0=sg_t[qb][:, 0:SN], scalar=banks_t[bV][:, SN:SN + 1],
                            in1=sg_t[qa][:, 0:SN], op0=ALU.add, op1=ALU.mult)),
                            reads=[sgb[qa], sgb[qb], bankb[bV]], writes=[tmpb[0]])
                        if s == 0:
                            P.emit("dve", (lambda e: e.tensor_tensor(
                                out=tmp[0][:, PAD:PAD + ICW], in0=tmp[0][:, PAD:PAD + ICW], in1=mask_t[:, 0:ICW],
                                op=ALU.mult)), reads=[tmpb[0], maskb], writes=[tmpb[0]])
                        P.emit("dve", (lambda e, bB=bB, o=o: e.tensor_tensor(
                            out=tmp[2][:, PAD + o:PAD + o + SN], in0=banks_t[bB][:, 0:SN], in1=rstd_t[:, o:o + SN],
                            op=ALU.mult)), reads=[bankb[bB], rstdb[s]], writes=[tmpb[2]])
                        P.emit("act", (lambda e, pb=pb, o=o: e.activation(
                            out=tmp[2][:, PAD + o:PAD + o + SN], in_=tmp[2][:, PAD + o:PAD + o + SN], func=AF.Identity,
                            bias=bias_t[:, 2 * pb + 1:2 * pb + 2])),
                            reads=[tmpb[2], biasb[pb]], writes=[tmpb[2]])
                    u = tmp[0]
                    y = tmp[1]
                    P.emit("dve", (lambda e, jj=jj: e.tensor_scalar(
                        out=y[:, PAD:PAD + TT], in0=u[:, PAD:PAD + TT], scalar1=vcol(V_CW(j, 2), jj), scalar2=None,
                        op0=ALU.mult)), reads=[tmpb[0], vecb], writes=[tmpb[1]])
                    for kk in (1, 2):
                        P.emit("dve", (lambda e, jj=jj, kk=kk: e.scalar_tensor_tensor(
                            out=y[:, PAD:PAD + TT], in0=u[:, PAD - kk:PAD - kk + TT], scalar=vcol(V_CW(j, 2 - kk), jj),
                            in1=y[:, PAD:PAD + TT], op0=ALU.mult, op1=ALU.add)),
                            reads=[tmpb[0], tmpb[1], vecb], writes=[tmpb[1]])
                    P.emit("dve", (lambda e, gb=gb, jl=jl: e.tensor_tensor(
                        out=a_t[:, gb, jl, :], in0=y[:, PAD:PAD + TT], in1=tmp[2][:, PAD:PAD + TT], op=ALU.mult)),
                        reads=[tmpb[1], tmpb[2]], writes=list(ab[gb][jl]))
                    if tsh == 0 and ("c", i, jj) in ADA_AT:
                        ada_flush()
                        ada_rows(*ADA_AT[("c", i, jj)])
                if q4 >= 1:
                    phase_b(q4 - 1)
            phase_b(3)

        def pool_mixer(i, tsh):
            j = i // 2
            all_ab = [b for gbl in ab for fl_ in gbl for b in fl_]
            bufsets = ((tmp[0], tmp[1], tmp[2], tmpb[0], tmpb[1], tmpb[2]),
                       (tmpB[0], tmpB[1], tmpB[2], tmpBb[0], tmpBb[1], tmpBb[2]))
            for q in range(3):
                P.emit("dve", (lambda e, q=q: e.memset(tmpB[q][:, 0:PAD], 0.0)), writes=all_ab + [tmpBb[q]])
            P.emit("dve", (lambda e: e.tensor_tensor(
                out=rstd_t[:, 0:ICW], in0=rstd_t[:, 0:ICW], in1=mask_t[:, 0:ICW], op=ALU.mult)),
                reads=[rstdb[0], maskb], writes=[rstdb[0]])

            def chain(k, c):
                g = k // 4
                w = POOL_W[g]
                hp, p0, p1, hpb, p0b, p1b = bufsets[c]
                ta = ta_t[:, c * ICW:(c + 1) * ICW]
                P.emit("dve", (lambda e: e.tensor_tensor(
                    out=hp[:, PAD:PAD + TT], in0=x_t[:, k, :], in1=rstd_t[:, :], op=ALU.mult)),
                    reads=list(xb[k]) + rstdb, writes=[hpb])
                yield
                cur, curb = hp, hpb
                sh = 1
                step = 0
                while sh < w:
                    nxt, nxtb = (p0, p0b) if step % 2 == 0 else (p1, p1b)
                    P.emit("dve", (lambda e, cur=cur, nxt=nxt, sh=sh: e.tensor_tensor(
                        out=nxt[:, PAD:PAD + TT], in0=cur[:, PAD:PAD + TT], in1=cur[:, PAD - sh:PAD - sh + TT],
                        op=ALU.add)), reads=[curb], writes=[nxtb])
                    yield
                    cur, curb = nxt, nxtb
                    sh *= 2
                    step += 1
                P.emit("dve", (lambda e, cur=cur: e.tensor_tensor(
                    out=ta, in0=cur[:, PAD:PAD + ICW], in1=inv_t[:, g * ICW:(g + 1) * ICW], op=ALU.mult)),
                    reads=[curb, invb], writes=[tab[c]])
                yield
                P.emit("dve", (lambda e: e.tensor_tensor(
                    out=ta, in0=ta, in1=hp[:, PAD:PAD + ICW], op=ALU.subtract)),
                    reads=[tab[c], hpb], writes=[tab[c]])
                yield
                P.emit("dve", (lambda e, cur=cur: e.scalar_tensor_tensor(
                    out=hp[:, PAD:PAD + TT], in0=cur[:, PAD:PAD + TT], scalar=1.0 / w, in1=hp[:, PAD:PAD + TT],
                    op0=ALU.mult, op1=ALU.subtract)),
                    reads=[curb, hpb], writes=[hpb])
                yield
                P.emit("dve", (lambda e: e.tensor_copy(out=hp[:, PAD:PAD + ICW], in_=ta)),
                       reads=[tab[c], hpb], writes=[hpb])
                yield
                P.emit("act", (lambda e: e.activation(
                    out=h_t[:, k, :, 0:SN], in_=hp[:, PAD:PAD + TT].rearrange("p (s n) -> p s n", s=NSUB),
                    func=AF.Identity, scale=dcol(i, 1, k))),
                    reads=[hpb, derb[i]], writes=list(hb[k]))
                yield

            def run_pair(k0):
                gens = [chain(k0, 0), chain(k0 + 1, 1)]
                live = [True, True]
                while any(live):
                    for c in range(2):
                        if live[c]:
                            try:
                                next(gens[c])
                            except StopIteration:
                                live[c] = False

            for g in range(4):
                run_pair(4 * g)
                run_pair(4 * g + 2)
                st, sbf = ws_get("pool_w")
                v = st[:, 0:2048].rearrange("p (f c) -> p f c", f=4, c=512)
                out_proj(v, sbf, 4,
                         lambda f, s, g=g: h_t[:, g * 4 + f, s, 0:SN],
                         lambda f, s, g=g: [hb[g * 4 + f][s]],
                         range(g * 4, g * 4 + 4),
                         lambda dc: dcol(i, 4, dc),
                         lambda vv, f, dc: vv[:, f, (dc % 4) * 128:(dc % 4 + 1) * 128], i)
                if tsh == 0 and ("p", i, g) in ADA_AT:
                    ada_flush()
                    ada_rows(*ADA_AT[("p", i, g)])
            for q in range(3):
                P.emit("dve", (lambda e, q=q: e.memset(tmpB[q][:, 0:PAD], 0.0)), writes=all_ab + [tmpBb[q]])

        for t in range(NT):
            xv = x_in[t].rearrange("(k p) n -> p k n", p=128)
            for k4 in range(0, NK, 4):
                P.dma("sp", (lambda e, k4=k4, xv=xv: e.dma_start(out=x_t[:, k4:k4 + 4, :], in_=xv[:, k4:k4 + 4, :])),
                      "xld", writes=[b for k in range(k4, k4 + 4) for b in xb[k]])
            P.dma("sp", (lambda e, t=t: e.dma_start(out=mask_t[:, :], in_=maskd[t])), "xld", writes=[maskb])
            P.dma("sp", (lambda e, t=t: e.dma_start(out=inv_t[:, :], in_=invd[t])), "xld", writes=[invb])
            for i in range(n_layers):
                if t == 0:
                    ada_derived(i, 0)
                h_shift(i, 0)
                norm_stats(lambda k, i=i: h_prime(i, 0, k))
                ffn(i, 0, t)
                if t == 0:
                    ada_derived(i, 1)
                if i % 2 == 0:
                    h_shift(i, 1)
                    norm_stats(lambda k, i=i: h_prime(i, 1, k))
                    conv_mixer(i, t)
                else:
                    norm_stats()
                    pool_mixer(i, t)
                if t == 0:
                    ada_derived(i, 2)
                h_shift(i, 2)
                norm_stats(lambda k, i=i: h_prime(i, 2, k))
                ffn(i, 1, t)
            norm_stats()
            for k in range(NK):
                q = k % 3
                P.emit("dve", (lambda e, q=q, k=k: e.tensor_tensor(
                    out=tmp[q][:, PAD:PAD + TT], in0=x_t[:, k, :], in1=rstd_t[:, :], op=ALU.mult)),
                    reads=list(xb[k]) + rstdb, writes=[tmpb[q]])
                P.emit("act", (lambda e, q=q, k=k: e.activation(
                    out=x_t[:, k, :], in_=tmp[q][:, PAD:PAD + TT], func=AF.Identity, scale=vcol(V_FIN, k))),
                    reads=[tmpb[q], vecb], writes=list(xb[k]))
            ov = out_d[t].rearrange("(k p) n -> p k n", p=128)
            for k4 in range(0, NK, 4):
                P.dma("sp", (lambda e, k4=k4, ov=ov: e.dma_start(out=ov[:, k4:k4 + 4, :], in_=x_t[:, k4:k4 + 4, HALO:HALO + OWN])),
                      "st", reads=[b for k in range(k4, k4 + 4) for b in xb[k]], writes=[Buf()])
        assert ws_state["next"] == len(plan), (ws_state, len(plan))
        final = [("st", P.dma_cnt["st"])] + [(e, P.cnt[e]) for e in ("pe", "act", "dve") if P.cnt[e] > 0]
        for k in P.dma_cnt:
            final.append((k, P.dma_cnt[k]))
        P.wait_all("sp", final)

        def run(eng, e):
            for waits, fn, incinfo in P.streams[eng]:
                for (k, v) in waits:
                    e.wait_ge(sems[k], v)
                if fn is None:
                    continue
                ins = fn(e)
                if incinfo is not None:
                    ins.then_inc(sems[incinfo[0]], incinfo[1])

        with nc.Block() as block:
            @block.tensor
            def _(e):
                run("pe", e)

            @block.scalar
            def _(e):
                run("act", e)

            @block.vector
            def _(e):
                run("dve", e)

            @block.gpsimd
            def _(e):
                run("pool", e)

            @block.sync
            def _(e):
                run("sp", e)
    return nc, {e: len(P.streams[e]) for e in P.ENGS}


def _cols(v):
    return np.ascontiguousarray(np.asarray(v, np.float32).reshape(NK, 128).T)


def _prep_inputs(x, c, norm_ffn1, norm_mix, norm_ffn2, w_ada, b_ada, w_ffn1_in, w_ffn1_out,
                 w_ffn2_in, w_ffn2_out, conv_in, conv_w, conv_out, pool_w, pool_scale, final_norm):
    x = np.asarray(x, np.float32)
    vec_list = []
    for i in range(DEPTH):
        vec_list += [_cols(norm_ffn1[i]), _cols(norm_mix[i]), _cols(norm_ffn2[i])]
    vec_list.append(_cols(final_norm))
    for j in range(2):
        for kk in range(3):
            vec_list.append(_cols(np.asarray(conv_w)[j, kk]))
    for j in range(2):
        vec_list.append(_cols(np.asarray(pool_scale)[j]))
    vecs = np.ascontiguousarray(np.concatenate(vec_list, axis=1))
    b_ada = np.asarray(b_ada, np.float32)
    bada = np.ascontiguousarray(np.concatenate(
        [b_ada[i].reshape(NMOD * NK, 128).T for i in range(DEPTH)], axis=1))
    shared = {
        "vecs": vecs, "bada": bada,
        "w_ada": np.asarray(w_ada, np.float32),
        "w_ffn1_in": np.asarray(w_ffn1_in, np.float32), "w_ffn1_out": np.asarray(w_ffn1_out, np.float32),
        "w_ffn2_in": np.asarray(w_ffn2_in, np.float32), "w_ffn2_out": np.asarray(w_ffn2_out, np.float32),
        "conv_in": np.asarray(conv_in, np.float32), "conv_out": np.asarray(conv_out, np.float32),
        "pool_w": np.asarray(pool_w, np.float32),
    }
    in_maps = []
    for core in range(NCORES):
        b = core // 2
        half = core % 2
        xin = np.zeros((NT, D, TT), np.float32)
        mask = np.zeros((NT, 128, TT), ml_dtypes.bfloat16)
        inv = np.ones((NT, 128, 4, ICW), np.float32)
        for t in range(NT):
            start = (half * NT + t) * OWN - HALO
            lo = max(start, 0)
            xin[t][:, lo - start:TTR] = x[b, lo:start + TTR, :].T
            mask[t][:, lo - start:] = 1.0
            pos = start + np.arange(ICW)
            for g, w in enumerate(POOL_W):
                cnt = np.minimum(np.maximum(pos, 0) + 1, w).astype(np.float32)
                inv[t][:, g, :] = (1.0 / cnt)[None, :]
        m = dict(shared)
        m["x_in"] = xin
        m["maskd"] = mask
        m["invd"] = np.ascontiguousarray(inv.reshape(NT, 128, 4 * ICW))
        m["c_col"] = _cols(np.asarray(c, np.float32)[b])
        in_maps.append(m)
    return in_maps


_CACHE = {}


def kernel(**inputs):
    in_maps = _prep_inputs(**inputs)
    if "nc" not in _CACHE:
        _CACHE["nc"] = build_program(DEPTH)[0]
    nc = _CACHE["nc"]
    res = run_bass_kernel_spmd(nc, in_maps, core_ids=list(range(NCORES)))
    out = np.empty((BATCH, SEQ, D), np.float32)
    for core in range(NCORES):
        b = core // 2
        half = core % 2
        o = res.results[core]["out"]
        for t in range(NT):
            s0 = (half * NT + t) * OWN
            out[b, s0:s0 + OWN, :] = o[t].T
    return out
```
